# Optimizing a Trainium2 kernel written in Bass

```python
import math
import jax
import jax.numpy as jnp
from jax import lax
import numpy as np

D_MODEL = 1024
BATCH = 8
SEQ = 8192
DEPTH = 1

N_MOD = 6
EPS = 1e-6
GM_CHUNK = 128
GM_HEAD = 128
GM_GROUPS = D_MODEL // GM_HEAD
GM_WIDTH = GM_GROUPS * GM_HEAD
SSM_INNER = 2 * D_MODEL
SSM_HEAD_DIM = 64
SSM_HEADS = SSM_INNER // SSM_HEAD_DIM
SSM_GROUPS = 8
SSM_STATE = 128
SSM_CONV = 4
SSM_CHUNK = 128
CONV_DIM = SSM_INNER + 2 * SSM_GROUPS * SSM_STATE
D_FF = 4 * D_MODEL
IN_SIZES = (GM_WIDTH, GM_WIDTH, SSM_INNER, CONV_DIM, SSM_HEADS, D_MODEL, D_MODEL)
IN_WIDTH = sum(IN_SIZES)

kernel_name = 'hybrid_sgu_ssd_block'


def _split_offsets(sizes):
    offs, acc = [], 0
    for s in sizes[:-1]:
        acc += s
        offs.append(acc)
    return offs


def rms_norm(x, w=None):
    xf = x.astype(jnp.float32)
    y = xf * lax.rsqrt(jnp.mean(jnp.square(xf), axis=-1, keepdims=True) + EPS)
    if w is not None:
        y = y * w.astype(jnp.float32)
    return y.astype(x.dtype)


def gated_group_rms_norm(y, z, w, groups):
    g = y.astype(jnp.float32) * jax.nn.silu(z.astype(jnp.float32))
    gs = g.reshape(*g.shape[:-1], groups, -1)
    gs = gs * lax.rsqrt(jnp.mean(jnp.square(gs), axis=-1, keepdims=True) + EPS)
    return (gs.reshape(g.shape) * w.astype(jnp.float32)).astype(y.dtype)


def sgu_branch(u, v, norm_w, ws, bs):
    b, s, _ = v.shape
    u = jax.nn.gelu(u)
    v = rms_norm(jax.nn.gelu(v), norm_w)
    vc = v.reshape(b, s // GM_CHUNK, GM_CHUNK, GM_GROUPS, GM_HEAD)
    causal = jnp.tril(jnp.ones((GM_CHUNK, GM_CHUNK), dtype=bool))
    ws_c = jnp.where(causal[None], ws, jnp.zeros_like(ws))
    sv = jnp.einsum('gij,bnjgd->bnigd', ws_c, vc) + bs.T[None, None, :, :, None]
    return u * sv.reshape(b, s, GM_WIDTH)


def ssd_scan(xs, dt, A, Bm, Cm):
    b, s, h, p = xs.shape
    g, n = Bm.shape[-2], Bm.shape[-1]
    hpg = h // g
    q = SSM_CHUNK
    nc = s // q

    def to_chunks(t):
        return jnp.swapaxes(t.reshape(b, nc, q, *t.shape[2:]), 0, 1)

    xdt = (xs.astype(jnp.float32) * dt[..., None]).reshape(b, s, g, hpg, p)
    a = (dt * A).reshape(b, s, g, hpg)
    causal = jnp.tril(jnp.ones((q, q), dtype=bool))

    def step(state, inp):
        x_c, a_c, B_c, C_c = inp
        cum = jnp.cumsum(a_c, axis=1)
        cum_t = jnp.moveaxis(cum, 1, -1)
        seg = cum_t[..., :, None] - cum_t[..., None, :]
        decay = jnp.exp(jnp.where(causal, seg, -jnp.inf))
        cb = jnp.einsum('bign,bjgn->bgij', C_c, B_c)
        y_intra = jnp.einsum('bgij,bghij,bjghp->bighp', cb, decay, x_c)
        y_inter = jnp.einsum('bign,bghpn->bighp', C_c, state) * jnp.exp(cum)[..., None]
        last = cum[:, -1]
        w_end = jnp.exp(last[:, None] - cum)
        new_state = state * jnp.exp(last)[..., None, None] + jnp.einsum(
            'bjgn,bjgh,bjghp->bghpn', B_c, w_end, x_c)
        return new_state, y_intra + y_inter

    state0 = jnp.zeros((b, g, hpg, p, n), jnp.float32)
    _, ys = lax.scan(step, state0, (to_chunks(xdt), to_chunks(a),
                                    to_chunks(Bm.astype(jnp.float32)),
                                    to_chunks(Cm.astype(jnp.float32))))
    return jnp.swapaxes(ys, 0, 1).reshape(b, s, h, p)


def ssd_branch(z, xbc, dt_raw, conv_w, conv_b, dt_bias, a_log, d_skip, norm_w):
    b, s, _ = xbc.shape
    xbc = lax.conv_general_dilated(
        xbc, conv_w[:, None, :], window_strides=(1,), padding=[(SSM_CONV - 1, 0)],
        dimension_numbers=('NWC', 'WIO', 'NWC'), feature_group_count=CONV_DIM) + conv_b
    xbc = jax.nn.silu(xbc)
    gn = SSM_GROUPS * SSM_STATE
    xs = xbc[..., :SSM_INNER].reshape(b, s, SSM_HEADS, SSM_HEAD_DIM)
    Bm = xbc[..., SSM_INNER:SSM_INNER + gn].reshape(b, s, SSM_GROUPS, SSM_STATE)
    Cm = xbc[..., SSM_INNER + gn:].reshape(b, s, SSM_GROUPS, SSM_STATE)
    dt = jax.nn.softplus(dt_raw.astype(jnp.float32) + dt_bias.astype(jnp.float32))
    A = -jnp.exp(a_log.astype(jnp.float32))
    y = ssd_scan(xs, dt, A, Bm, Cm)
    y = y + xs.astype(jnp.float32) * d_skip.astype(jnp.float32)[:, None]
    y = y.reshape(b, s, SSM_INNER).astype(z.dtype)
    return gated_group_rms_norm(y, z, norm_w, SSM_GROUPS)


def setup_inputs(seed: int = 0) -> dict:
    key = jax.random.key(seed)
    ks = jax.random.split(key, 22)
    f = jnp.float32
    L = DEPTH

    def nrm(k, shape, fan_in):
        return jax.random.normal(k, shape, f) * (fan_in ** -0.5)

    x = jax.random.normal(ks[0], (BATCH, SEQ, D_MODEL), f)
    c = jax.random.normal(ks[1], (BATCH, D_MODEL), f)
    w_mod = 0.5 * nrm(ks[2], (L, D_MODEL, N_MOD * D_MODEL), D_MODEL)
    b_mod = 0.01 * jax.random.normal(ks[3], (L, N_MOD * D_MODEL), f)
    w_in = nrm(ks[4], (L, D_MODEL, IN_WIDTH), D_MODEL)
    gm_norm_w = 1.0 + 0.05 * jax.random.normal(ks[5], (L, GM_WIDTH), f)
    gm_ws = nrm(ks[6], (L, GM_GROUPS, GM_CHUNK, GM_CHUNK), GM_CHUNK)
    gm_bs = 1.0 + 0.1 * jax.random.normal(ks[7], (L, GM_GROUPS, GM_CHUNK), f)
    conv_w = nrm(ks[8], (L, SSM_CONV, CONV_DIM), SSM_CONV)
    conv_b = 0.01 * jax.random.normal(ks[9], (L, CONV_DIM), f)
    dt0 = jnp.exp(jax.random.uniform(ks[10], (L, SSM_HEADS), f,
                                     minval=math.log(1e-3), maxval=math.log(1e-1)))
    dt_bias = dt0 + jnp.log(-jnp.expm1(-dt0))
    a_log = jnp.log(jax.random.uniform(ks[11], (L, SSM_HEADS), f, minval=1.0, maxval=16.0))
    d_skip = 1.0 + 0.1 * jax.random.normal(ks[12], (L, SSM_HEADS), f)
    ssm_norm_w = 1.0 + 0.05 * jax.random.normal(ks[13], (L, SSM_INNER), f)
    w_branch_gm = nrm(ks[14], (L, GM_WIDTH, D_MODEL), GM_WIDTH)
    w_branch_ssm = nrm(ks[15], (L, SSM_INNER, D_MODEL), SSM_INNER)
    w_out = nrm(ks[16], (L, D_MODEL, D_MODEL), D_MODEL)
    w_ff1 = nrm(ks[17], (L, D_MODEL, D_FF), D_MODEL)
    w_ff2 = nrm(ks[18], (L, D_FF, D_MODEL), D_FF)
    final_norm_w = 1.0 + 0.05 * jax.random.normal(ks[19], (D_MODEL,), f)
    return {'x': x, 'c': c, 'w_mod': w_mod, 'b_mod': b_mod, 'w_in': w_in,
            'gm_norm_w': gm_norm_w, 'gm_ws': gm_ws, 'gm_bs': gm_bs,
            'conv_w': conv_w, 'conv_b': conv_b, 'dt_bias': dt_bias, 'a_log': a_log,
            'd_skip': d_skip, 'ssm_norm_w': ssm_norm_w, 'w_branch_gm': w_branch_gm,
            'w_branch_ssm': w_branch_ssm, 'w_out': w_out, 'w_ff1': w_ff1,
            'w_ff2': w_ff2, 'final_norm_w': final_norm_w}


def reference(x, c, w_mod, b_mod, w_in, gm_norm_w, gm_ws, gm_bs, conv_w, conv_b,
              dt_bias, a_log, d_skip, ssm_norm_w, w_branch_gm, w_branch_ssm, w_out,
              w_ff1, w_ff2, final_norm_w):
    c_act = jax.nn.silu(c)
    offs = _split_offsets(IN_SIZES)
    for l in range(DEPTH):
        mod = (c_act @ w_mod[l] + b_mod[l])[:, None, :]
        sh1, sc1, g1, sh2, sc2, g2 = jnp.split(mod, N_MOD, axis=-1)
        h = rms_norm(x) * (1.0 + sc1) + sh1
        proj = h @ w_in[l]
        u, v, z, xbc, dt_raw, gate_a, gate_b = jnp.split(proj, offs, axis=-1)
        y_a = sgu_branch(u, v, gm_norm_w[l], gm_ws[l], gm_bs[l])
        y_b = ssd_branch(z, xbc, dt_raw, conv_w[l], conv_b[l], dt_bias[l], a_log[l],
                         d_skip[l], ssm_norm_w[l])
        mixed = (jax.nn.sigmoid(gate_a) * (y_a @ w_branch_gm[l])
                 + jax.nn.sigmoid(gate_b) * (y_b @ w_branch_ssm[l]))
        x = x + g1 * (mixed @ w_out[l])
        h2 = rms_norm(x) * (1.0 + sc2) + sh2
        x = x + g2 * (jnp.square(jax.nn.relu(h2 @ w_ff1[l])) @ w_ff2[l])
    return rms_norm(x, final_norm_w)
```

```python
import contextlib
import numpy as np
import concourse.bass as bass
import concourse.mybir as mybir
from concourse.bass_utils import run_bass_kernel_spmd

F32 = mybir.dt.float32
BF16 = mybir.dt.bfloat16
AF = mybir.ActivationFunctionType
ALU = mybir.AluOpType
AX = mybir.AxisListType

D = 1024
NKT = 8
DIN = 2048
NH = 32
NG = 8
DFF = 4096
EPS = 1e-6
OFF_U, OFF_V, OFF_Z, OFF_XBC, OFF_DT, OFF_GA, OFF_GB = 0, 1024, 2048, 4096, 8192, 8224, 9248
N_CORES = 8
DEBUG_BARRIER = False
DEBUG_DUMP = False
OVERLAP = True
NFILL = 0

ENG_NAMES = ("pe", "act", "dve", "pool", "sp")
EPOCH = 12000


class Op:
    __slots__ = ("eng", "fn", "reads", "writes", "dma", "semkey", "idx", "deps", "sig", "cnt", "dcount", "waits")

    def __init__(self, eng, fn, reads, writes, dma, semkey):
        self.eng = eng
        self.fn = fn
        self.reads = reads
        self.writes = writes
        self.dma = dma
        self.semkey = semkey
        self.deps = []
        self.sig = False
        self.cnt = 0
        self.dcount = 0
        self.waits = []


class Sched:
    def __init__(self):
        self.ops = []
        self.streams = {e: [] for e in ENG_NAMES}
        self.last_write = {}
        self.readers = {}
        self.dma_counts = {}

    def add(self, eng, fn, reads=(), writes=(), dma=False, semkey=None):
        op = Op(eng, fn, tuple(reads), tuple(writes), dma, semkey)
        if dma:
            assert semkey is not None
            n = self.dma_counts.get(semkey, 0) + 1
            self.dma_counts[semkey] = n
            op.dcount = n
        op.idx = len(self.streams[eng])
        self.streams[eng].append(op)
        self.ops.append(op)
        deps = {}
        for r in op.reads:
            w = self.last_write.get(r)
            if w is not None:
                deps[id(w)] = (w, True)
        for r in op.writes:
            w = self.last_write.get(r)
            if w is not None and id(w) not in deps:
                deps[id(w)] = (w, False)
            for rd in self.readers.get(r, ()):
                if id(rd) not in deps:
                    deps[id(rd)] = (rd, False)
        op.deps = list(deps.values())
        for r in op.reads:
            lst = self.readers.setdefault(r, [])
            if not dma:
                lst[:] = [o for o in lst if o.dma or o.eng != eng]
            lst.append(op)
        for r in op.writes:
            self.last_write[r] = op
            self.readers[r] = []
        return op

    def barrier(self):
        lasts = []
        for e in ENG_NAMES:
            comp = [o for o in self.streams[e] if not o.dma and o.fn is not None]
            if comp:
                lasts.append((comp[-1], True))
        lastd = {}
        for o in self.ops:
            if o.dma:
                lastd[o.semkey] = o
        lasts += [(o, True) for o in lastd.values()]
        for e in ENG_NAMES:
            op = Op(e, None, (), (), False, None)
            op.idx = len(self.streams[e])
            self.streams[e].append(op)
            self.ops.append(op)
            op.deps = [(p, True) for (p, _) in lasts]

    def finalize(self):
        seen = {e: {} for e in ENG_NAMES}
        for op in self.ops:
            need = {}
            for (p, raw) in op.deps:
                if p.dma:
                    k = ("d", p.semkey)
                    v = p.dcount
                else:
                    if p.eng == op.eng and not op.dma and op.fn is not None:
                        if p.eng == "pe":
                            continue
                        if not raw:
                            continue
                    k = ("e", p.eng)
                    v = p.idx + 1
                if seen[op.eng].get(k, 0) >= v:
                    continue
                if need.get(k, (0, None))[0] < v:
                    need[k] = (v, p)
            for k, (v, p) in need.items():
                seen[op.eng][k] = v
                p.sig = True
            op.waits = list(need.items())
        self.nsig = {}
        for e in ENG_NAMES:
            c = 0
            for op in self.streams[e]:
                if op.dma:
                    continue
                if op.sig:
                    c += 1
                    op.cnt = c
            self.nsig[e] = c

    def emit(self, nc):
        self.finalize()
        with contextlib.ExitStack() as st:
            esems = {}
            for e in ENG_NAMES:
                n = self.nsig[e] // EPOCH + 1
                esems[e] = [st.enter_context(nc.semaphore(f"s_{e}_{i}")) for i in range(n)]
            dsems = {}
            for k in self.dma_counts:
                dsems[k] = st.enter_context(nc.semaphore("d_" + str(len(dsems))))
            self.n_sems = sum(len(v) for v in esems.values()) + len(dsems)
            block = st.enter_context(nc.Block())

            def run_stream(ename):
                def body(eng):
                    for op in self.streams[ename]:
                        for k, (v, p) in op.waits:
                            if k[0] == "d":
                                eng.wait_ge(dsems[k[1]], 16 * p.dcount)
                            else:
                                c = p.cnt
                                ep = (c - 1) // EPOCH
                                eng.wait_ge(esems[k[1]][ep], c - ep * EPOCH)
                        if op.fn is None:
                            continue
                        ins = op.fn(eng)
                        if op.dma:
                            ins.then_inc(dsems[op.semkey], 16)
                        elif op.sig:
                            ep = (op.cnt - 1) // EPOCH
                            ins.then_inc(esems[ename][ep], 1)
                return body

            block.tensor(run_stream("pe"))
            block.scalar(run_stream("act"))
            block.vector(run_stream("dve"))
            block.gpsimd(run_stream("pool"))
            block.sync(run_stream("sp"))


class Ring:
    def __init__(self, name, tensors):
        self.name = name
        self.tensors = tensors
        self.i = 0

    def get(self):
        i = self.i
        self.i = (i + 1) % len(self.tensors)
        return self.tensors[i], (self.name, i)


class WRing:
    def __init__(self, name, tensors):
        self.name = name
        self.tensors = tensors
        self.free = list(range(len(tensors)))

    def get(self):
        assert self.free, "weight ring exhausted"
        i = self.free.pop(0)
        return self.tensors[i], (self.name, i)

    def release(self, key):
        assert key[1] not in self.free
        self.free.append(key[1])


SLOTS = {}
_n = 0
for _nm, _c in [("u", 2), ("xbc", 8), ("v", 2), ("z", 4), ("ga", 2), ("gb", 2), ("gm", 2), ("ssm", 4),
                ("out", 2), ("ff1", 8), ("ff2", 8)]:
    SLOTS[_nm] = (_n, _c)
    _n += _c
NSLOT = _n

PP_C, PP_CW, PP_CB, PP_NW, PP_DS, PP_N = 0, 8, 136, 168, 184, 200
BP_DTB, BP_ALOG, BP_GMNW, BP_FNW, BP_N = 0, 32, 64, 1088, 2112


def build_nc(S, T):
    NCH = T // 128
    assert T == 256
    NBLK = S // T
    assert S % T == 0 and T % 128 == 0
    nc = bass.Bass("TRN2", target_bir_lowering=False)

    def din(name, shape, dt=F32):
        return nc.dram_tensor(name, shape, dt, kind="ExternalInput").ap()

    x_d = din("x", [S, D])
    wmod_d = din("w_mod", [D, 6 * D])
    bmod_d = din("bmod", [128, 6 * D])
    win_d = din("w_in", [D, 10272])
    wgm_d = din("w_gm", [D, D])
    wssm_d = din("w_ssm", [DIN, D])
    wout_d = din("w_out", [D, D])
    wff1_d = din("w_ff1", [D, DFF])
    wff2_d = din("w_ff2", [DFF, D])
    consts_d = din("consts", [128, 4, 128])
    pp_d = din("pp", [128, PP_N])
    bp_d = din("bp", [128, BP_N])
    bsrow_d = din("bsrow", [8, 128])
    wsT_d = din("wsT", [128, 8, 128])
    out_d = nc.dram_tensor("out", [S, D], F32, kind="ExternalOutput").ap()
    wsl_d = nc.dram_tensor("wsl", [NSLOT, 128, 4096], BF16, kind="Internal").ap()
    wdt_d = nc.dram_tensor("wdt_s", [128, 256], BF16, kind="Internal").ap()

    S_ = Sched()
    A = S_.add
    dbg = {}

    def dump(name, ap, keys, shape, dt=F32):
        if not DEBUG_DUMP or name in dbg:
            return
        dbg[name] = nc.dram_tensor("dbg_" + name, list(shape), dt, kind="ExternalOutput").ap()
        A("sp", lambda e: e.dma_start(out=dbg[name], in_=ap), reads=keys, dma=True, semkey=("dbg", name))

    with contextlib.ExitStack() as st:
        def sb(name, shape, dt=F32):
            return st.enter_context(nc.sbuf_tensor("s_" + name, shape, dt))

        consts = sb("consts", [128, 4, 128])
        ident_f = consts[:, 0, :]
        Umat = consts[:, 1, :]
        Vmat = consts[:, 2, :]
        ones_f = consts[:, 3, :]
        cbf = sb("cbf", [128, 4, 128], BF16)
        ident_bf = cbf[:, 0, :]
        Ubf = cbf[:, 1, :]
        Vbf = cbf[:, 2, :]
        ones_bf = cbf[:, 3, :]
        pp = sb("pp", [128, PP_N])
        bp = sb("bp", [128, BP_N])
        bsmat = sb("bsmat", [8, 2, 128], BF16)
        wsTc = sb("wsTc", [128, 8, 128], BF16)
        diagD = sb("diagD", [128, 16, 128], BF16)
        wdt = sb("wdt", [128, 8, 32], BF16)
        negA = sb("negA", [128, 32])
        cact = sb("cact", [128, 8])
        modp = sb("modp", [128, 4, 8])
        mhalf = sb("mhalf", [128, 1])
        epsc = sb("epsc", [128, 1])
        Sst = sb("Sst", [128, 2048])
        Sbf = sb("Sbf", [128, 2048], BF16)
        tails = sb("tails", [128, 32, 4])
        xres = sb("xres", [128, NCH, D])
        hT = sb("hT", [128, NKT, T], BF16)
        h2T = sb("h2T", [128, NKT, T], BF16)
        fT = sb("fT", [128, 32, T], BF16)
        guT = sb("guT", [128, NKT, T], BF16)
        big = sb("big", [128, 32, T], BF16)
        vn = sb("vn", [128, NCH, D], BF16)
        sz = sb("sz", [128, NCH, DIN], BF16)
        dtx = sb("dtx", [128, NCH, 32])
        dtt = sb("dtt", [128, NCH, 32])
        dtl = sb("dtl", [128, NCH, 32])
        a_all = sb("a_all", [128, NCH, 32])
        y_aT = sb("y_aT", [128, NKT, T], BF16)
        y_bT = sb("y_bT", [128, 16, T], BF16)
        xs_tok = sb("xs_tok", [128, DIN], BF16)
        B_tok = sb("B_tok", [128, 1024], BF16)
        xw = sb("xw", [128, DIN], BF16)
        cl_buf = sb("cl_buf", [128, 64])
        ecl_buf = sb("ecl_buf", [128, 64])
        wend_buf = sb("wend_buf", [128, 64])
        a_hi = sb("a_hi", [128, NCH, 32], BF16)
        aUh = sb("aUh", [128, 1024], BF16)
        cbm = sb("cbm", [128, 1024], BF16)
        jk2 = sb("jk2", [128, 512])
        r2k = Ring("r2k", [sb(f"r2k{i}", [128, 512]) for i in range(2)])
        r2k_y = Ring("r2ky", [sb(f"r2ky{i}", [128, 512]) for i in range(1)])
        rg = Ring("rg", [sb(f"rg{i}", [128, 256]) for i in range(8)])
        rT_y = Ring("rTy", [sb(f"rTy{i}", [128, 256]) for i in range(2)])
        rE = Ring("rE", [sb(f"rE{i}", [128, 512]) for i in range(2)])
        rAT = Ring("rAT", [sb(f"rAT{i}", [128, 512], BF16) for i in range(3)])
        ryi = Ring("ryi", [sb(f"ryi{i}", [128, 256]) for i in range(3)])
        rgs = Ring("rgs", [sb(f"rgs{i}", [128, 256], BF16) for i in range(3)])
        r4k = Ring("r4k", [sb(f"r4k{i}", [128, 1024]) for i in range(2)])
        rraw = Ring("raw", [sb(f"raw{i}", [128, T + 4]) for i in range(3)])
        rst = Ring("st", [sb(f"st{i}", [128, 64]) for i in range(8)])
        rst_y = Ring("sty", [sb(f"sty{i}", [128, 64]) for i in range(4)])
        wring = WRing("wr", [sb(f"wr{i}", [128, 8, 512], BF16) for i in range(5)])
        ps = st.enter_context(nc.psum_tensor("ps", [128, 8, 512], F32))
        print("sbuf bytes remaining:", nc.sbuf_bytes_remaining)

        bank_ptr = [0]
        NXB = 5

        def alloc(n=1):
            b = bank_ptr[0]
            if b + n > NXB:
                b = 0
            bank_ptr[0] = (b + n) % NXB
            return b

        bank_ptr_y = [0]

        def alloc_y(n=1):
            b = bank_ptr_y[0]
            if b + n > 3:
                b = 0
            bank_ptr_y[0] = (b + n) % 3
            return 5 + b

        def pk(b, n=1):
            return [("ps", b + i) for i in range(n)]

        def dma_in(eng, out_ap, in_ap, wkeys, semkey, rkeys=()):
            A(eng, lambda e: e.dma_start(out=out_ap, in_=in_ap), reads=rkeys, writes=wkeys, dma=True, semkey=semkey)

        dma_in("sp", consts[:], consts_d, ["consts"], "c_consts")
        dma_in("sp", pp[:], pp_d, ["pp"], "c_pp")
        dma_in("sp", bp[:], bp_d, ["bp"], "c_bp")
        bs_st, bs_k = r4k.get()
        dma_in("sp", bs_st[0:8, 0:128], bsrow_d, [bs_k], "c_bsrow")
        wsT_st, wsT_k = r4k.get()
        dma_in("sp", wsT_st[:].rearrange("p (g i) -> p g i", g=8), wsT_d, [wsT_k], "c_wsT")

        def cast_slot(slot, W, r0, c0):
            src = W[r0:r0 + 1024, c0:c0 + 512].rearrange("(k p) n -> p k n", p=128)
            dst = wsl_d[slot].rearrange("p (k n) -> p k n", k=8)
            A("pool", lambda e: e.dma_start(out=dst, in_=src), writes=[("wsl", slot)], dma=True, semkey=("wsl", slot))

        for j in range(2):
            cast_slot(SLOTS["u"][0] + j, win_d, 0, OFF_U + 512 * j)
        for j in range(8):
            cast_slot(SLOTS["xbc"][0] + j, win_d, 0, OFF_XBC + 512 * j)
        for j in range(2):
            cast_slot(SLOTS["v"][0] + j, win_d, 0, OFF_V + 512 * j)
        for j in range(4):
            cast_slot(SLOTS["z"][0] + j, win_d, 0, OFF_Z + 512 * j)
        A("pool", lambda e: e.dma_start(out=wdt_d.rearrange("p (k n) -> p k n", k=8),
                                        in_=win_d[:, OFF_DT:OFF_DT + 32].rearrange("(k p) n -> p k n", p=128)),
          writes=["wdt_d"], dma=True, semkey="wdt_d")
        dma_in("sp", wdt[:], wdt_d.rearrange("p (k n) -> p k n", k=8), ["wdt"], "c_wdt", rkeys=["wdt_d"])
        for j in range(2):
            cast_slot(SLOTS["ga"][0] + j, win_d, 0, OFF_GA + 512 * j)
        for j in range(2):
            cast_slot(SLOTS["gm"][0] + j, wgm_d, 0, 512 * j)
        for j in range(2):
            cast_slot(SLOTS["gb"][0] + j, win_d, 0, OFF_GB + 512 * j)
        for j in range(2):
            for kh in range(2):
                cast_slot(SLOTS["ssm"][0] + 2 * j + kh, wssm_d, 1024 * kh, 512 * j)
        for j in range(8):
            cast_slot(SLOTS["ff1"][0] + j, wff1_d, 0, 512 * j)

        A("dve", lambda e: e.tensor_copy(out=cbf[:], in_=consts[:]), reads=["consts"], writes=["ident_bf"])
        A("dve", lambda e: e.tensor_copy(out=bsmat[:, 0, :], in_=bs_st[0:8, 0:128]), reads=[bs_k], writes=["bsH"])
        A("dve", lambda e: e.tensor_tensor(out=bsmat[:, 1, :], in0=bs_st[0:8, 0:128], in1=bsmat[:, 0, :], op=ALU.subtract),
          reads=[bs_k, "bsH"], writes=["bsH"])
        A("dve", lambda e: e.tensor_tensor(out=wsTc[:], in0=wsT_st[:].rearrange("p (g i) -> p g i", g=8),
                                           in1=Umat.unsqueeze(1).broadcast_to([128, 8, 128]), op=ALU.mult),
          reads=[wsT_k, "consts"], writes=["wsTc"])
        A("dve", lambda e: e.tensor_tensor(out=diagD[:], in0=ident_f.unsqueeze(1).broadcast_to([128, 16, 128]),
                                           in1=pp[:, PP_DS:PP_DS + 16].unsqueeze(2).broadcast_to([128, 16, 128]), op=ALU.mult),
          reads=["consts", "pp"], writes=["diagD"])
        A("act", lambda e: e.activation(out=negA[:], in_=bp[:, BP_ALOG:BP_ALOG + 32], func=AF.Exp), reads=["bp"], writes=["negA"])
        A("dve", lambda e: e.tensor_scalar(out=negA[:], in0=negA[:], scalar1=-1.0, scalar2=None, op0=ALU.mult),
          reads=["negA"], writes=["negA"])
        A("act", lambda e: e.activation(out=cact[:], in_=pp[:, PP_C:PP_C + 8], func=AF.Silu), reads=["pp"], writes=["cact"])
        A("pool", lambda e: e.memset(mhalf[:], -0.5), writes=["mhalf"])
        A("pool", lambda e: e.memset(epsc[:], EPS), writes=["mhalf"])
        A("pool", lambda e: e.memset(Sst[:], 0.0), writes=["Sst"])
        A("pool", lambda e: e.memset(Sbf[:], 0.0), writes=["Sbf"])
        A("pool", lambda e: e.memset(tails[:], 0.0), writes=["tails"])

        big_f = big[:].rearrange("p a b -> p (a b)").bitcast(F32)
        nstage = (16 * T) // 4096
        assert nstage >= 1
        stage_keys = []
        tiles_per_stage = 32 // nstage
        for i in range(nstage):
            stage_keys.append([("big", t) for t in range(i * tiles_per_stage, (i + 1) * tiles_per_stage)])
        stage_i = [0]

        def get_stage():
            i = stage_i[0]
            stage_i[0] = (i + 1) % nstage
            return big_f[:, i * 4096:(i + 1) * 4096].rearrange("p (k n) -> p k n", k=8), stage_keys[i], ("stage", i)

        gB = [xres[:, 0, :], xres[:, 1 % NCH, :]]
        gBk = [("xres", 0), ("xres", 1 % NCH)]
        if NCH == 1:
            raise AssertionError("need NCH>=2")
        for nt in range(12):
            sec = nt // 2
            half = nt % 2
            stg, stg_keys, stg_sem = get_stage()
            dma_in("sp", stg, wmod_d[:, nt * 512:(nt + 1) * 512].rearrange("(k p) n -> p k n", p=128), stg_keys, stg_sem)
            bm, bm_k = r2k.get()
            dma_in("sp", bm[:], bmod_d[:, nt * 512:(nt + 1) * 512], [bm_k], ("bm", bm_k[1]))
            b = alloc()
            for k in range(8):
                A("pe", lambda e, k=k, b=b, stg=stg: e.matmul(ps[:, b, :], lhsT=cact[:, k:k + 1].broadcast_to([128, 128]),
                                                              rhs=stg[:, k, :], start=(k == 0), stop=(k == 7)),
                  reads=["cact"] + stg_keys, writes=pk(b))
            if sec in (2, 5):
                gi = 0 if sec == 2 else 1
                A("dve", lambda e, b=b, bm=bm, gi=gi, half=half: e.tensor_tensor(out=gB[gi][:, half * 512:(half + 1) * 512],
                                                                               in0=ps[:, b, :], in1=bm[:], op=ALU.add),
                  reads=pk(b) + [bm_k], writes=[gBk[gi]])
            else:
                col = {0: 0, 1: 1, 3: 2, 4: 3}[sec]
                tmp, tmp_k = r2k.get()
                A("dve", lambda e, b=b, bm=bm, tmp=tmp: e.tensor_tensor(out=tmp[:], in0=ps[:, b, :], in1=bm[:], op=ALU.add),
                  reads=pk(b) + [bm_k], writes=[tmp_k])
                A("dve", lambda e, tmp=tmp: e.tensor_tensor(out=tmp[:].rearrange("p (a i) -> p a i", a=4),
                                                            in0=tmp[:].rearrange("p (a i) -> p a i", a=4),
                                                            in1=ident_f.unsqueeze(1).broadcast_to([128, 4, 128]), op=ALU.mult),
                  reads=[tmp_k, "consts"], writes=[tmp_k])
                A("dve", lambda e, tmp=tmp, col=col, half=half: e.reduce_sum(out=modp[:, col, half * 4:(half + 1) * 4],
                                                                          in_=tmp[:].rearrange("p (a i) -> p a i", a=4), axis=AX.X),
                  reads=[tmp_k], writes=["modp"])
        for col in (1, 3):
            A("dve", lambda e, col=col: e.tensor_scalar(out=modp[:, col, :], in0=modp[:, col, :], scalar1=1.0, scalar2=None, op0=ALU.add),
              reads=["modp"], writes=["modp"])

        def fold_slot(slot, W, r0, c0, gi):
            stg, stg_keys, stg_sem = get_stage()
            dma_in("sp", stg, W[r0:r0 + 1024, c0:c0 + 512].rearrange("(k p) n -> p k n", p=128), stg_keys, stg_sem)
            wt, wt_k = wring.get()
            A("dve", lambda e: e.tensor_tensor(out=wt[:], in0=stg,
                                               in1=gB[gi][:, c0:c0 + 512].unsqueeze(1).broadcast_to([128, 8, 512]), op=ALU.mult),
              reads=stg_keys + [gBk[gi]], writes=[wt_k])
            A("sp", lambda e: e.dma_start(out=wsl_d[slot].rearrange("p (k n) -> p k n", k=8), in_=wt[:]),
              reads=[wt_k], writes=[("wsl", slot)], dma=True, semkey=("wsl", slot))
            wring.release(wt_k)

        for j in range(2):
            fold_slot(SLOTS["out"][0] + j, wout_d, 0, 512 * j, 0)
        for n in range(2):
            for kg in range(4):
                fold_slot(SLOTS["ff2"][0] + 4 * n + kg, wff2_d, 1024 * kg, 512 * n, 1)

        if DEBUG_BARRIER:
            S_.barrier()
        def wload(slot):
            wt, wt_k = wring.get()
            A("sp", lambda e: e.dma_start(out=wt[:].rearrange("p k n -> p (k n)"), in_=wsl_d[slot]),
              reads=[("wsl", slot)], writes=[wt_k], dma=True, semkey=wt_k)
            return wt, wt_k

        class Prefetch:
            def __init__(self):
                sl = [SLOTS["u"][0], SLOTS["u"][0] + 1, SLOTS["v"][0], SLOTS["v"][0] + 1]
                sl += [SLOTS["xbc"][0] + j for j in range(8)] + [SLOTS["z"][0] + j for j in range(4)]
                for j in range(2):
                    sl += [SLOTS["ga"][0] + j, SLOTS["gm"][0] + j, SLOTS["gb"][0] + j, SLOTS["ssm"][0] + 2 * j, SLOTS["ssm"][0] + 2 * j + 1]
                self.slots = sl
                self.loaded = {}
                self.nopen = 0

            def get(self, i):
                for j in (i, i + 1, i + 2):
                    if j < len(self.slots) and j not in self.loaded and (j == i or self.nopen < 3):
                        self.loaded[j] = wload(self.slots[j])
                        self.nopen += 1
                return self.loaded[i]

            def done(self, i):
                wring.release(self.loaded[i][1])
                self.nopen -= 1

        def rstd_from_ss(ss, ss_k, n_feat, ring=None):
            r, r_k = (ring or rst).get()
            A("pool", lambda e: e.tensor_scalar(out=r[:, 0:1], in0=ss, scalar1=1.0 / n_feat, scalar2=EPS, op0=ALU.mult, op1=ALU.add),
              reads=[ss_k], writes=[r_k])
            A("pool", lambda e: e.tensor_tensor(out=r[:, 1:2], in0=r[:, 0:1], in1=mhalf[:], op=ALU.pow),
              reads=[r_k, "mhalf"], writes=[r_k])
            return r[:, 1:2], r_k

        jk2b = jk2[:].bitcast(BF16)

        def norm_multi(jobs):
            norm_tail(jobs, norm_head(jobs))

        def norm_head(jobs):
            st_ = []
            for (c, src, src_k, cs, cb, dst, dst_key, yth) in jobs:
                ring_s = rst_y if yth else rst
                ss, ss_k = ring_s.get()
                A("act", lambda e, src=src, ss=ss: e.activation(out=jk2b, in_=src, func=AF.Square, accum_out=ss[:, 0:1]),
                  reads=[src_k], writes=["jk2", ss_k])
                st_.append([ss, ss_k, ring_s])
            for i, (c, src, src_k, cs, cb, dst, dst_key, yth) in enumerate(jobs):
                ss, ss_k, ring_s = st_[i]
                rs, rs_k = rstd_from_ss(ss[:, 0:1], ss_k, D, ring_s)
                st_[i] += [rs, rs_k]
            for i, (c, src, src_k, cs, cb, dst, dst_key, yth) in enumerate(jobs):
                rs, rs_k = st_[i][3], st_[i][4]
                xb, xb_k = (r2k_y if yth else r2k).get()
                xbv = xb[:].bitcast(BF16)
                A("dve", lambda e, xbv=xbv, src=src, rs=rs: e.tensor_scalar(out=xbv, in0=src, scalar1=rs, scalar2=None, op0=ALU.mult),
                  reads=[src_k, rs_k], writes=[xb_k])
                st_[i] += [xbv, xb_k]
            return st_

        def norm_tail(jobs, st_):
            for i, (c, src, src_k, cs, cb, dst, dst_key, yth) in enumerate(jobs):
                xbv, xb_k = st_[i][5], st_[i][6]
                b = alloc_y() if yth else alloc()
                psb = ps[:, b, :].bitcast(BF16)
                for k in range(8):
                    A("pe", lambda e, k=k, psb=psb, xbv=xbv: e.transpose(out=psb[:, k * 128:(k + 1) * 128], in_=xbv[:, k * 128:(k + 1) * 128], identity=ident_bf),
                      reads=[xb_k, "ident_bf"], writes=pk(b))
                st_[i] += [b, psb]
            for i, (c, src, src_k, cs, cb, dst, dst_key, yth) in enumerate(jobs):
                b, psb = st_[i][7], st_[i][8]
                if yth:
                    for k in range(8):
                        A("dve", lambda e, k=k, psb=psb, dst=dst, c=c, cs=cs, cb=cb: e.tensor_scalar(out=dst[:, k, c * 128:(c + 1) * 128], in0=psb[:, k * 128:(k + 1) * 128],
                                                                                                     scalar1=modp[:, cs, k:k + 1], scalar2=modp[:, cb, k:k + 1],
                                                                                                     op0=ALU.mult, op1=ALU.add),
                          reads=pk(b) + ["modp"], writes=[dst_key])
                    continue
                for k in range(8):
                    A("act", lambda e, k=k, psb=psb, dst=dst, c=c, cs=cs, cb=cb: e.activation(out=dst[:, k, c * 128:(c + 1) * 128], in_=psb[:, k * 128:(k + 1) * 128], func=AF.Identity,
                                                                                              scale=modp[:, cs, k:k + 1], bias=modp[:, cb, k:k + 1]),
                      reads=pk(b) + ["modp"], writes=[dst_key])

        hT_keys = [("hT", c) for c in range(NCH)]

        def fm_matmuls(W, W_k, f, rhs_buf, rhs_keys, b):
            for k in range(8):
                A("pe", lambda e, k=k: e.matmul(ps[:, b, 0:T], lhsT=W[:, k, f * 128:(f + 1) * 128], rhs=rhs_buf[:, k, :],
                                                start=(k == 0), stop=(k == 7)),
                  reads=[W_k] + rhs_keys, writes=pk(b))

        def conv_tile_front(W, W_k, f, t):
            b = alloc()
            fm_matmuls(W, W_k, f, hT, hT_keys, b)
            raw, raw_k = rraw.get()
            acc, acc_k = rg.get()
            A("act", lambda e: e.activation(out=raw[:, 4:4 + T], in_=ps[:, b, 0:T], func=AF.Copy), reads=pk(b), writes=[raw_k])
            A("act", lambda e: e.activation(out=acc[:, 0:T], in_=ps[:, b, 0:T], func=AF.Identity,
                                            scale=pp[:, PP_CW + 4 * t + 3:PP_CW + 4 * t + 4], bias=pp[:, PP_CB + t:PP_CB + t + 1]),
              reads=pk(b) + ["pp"], writes=[acc_k])
            A("pool", lambda e: e.tensor_copy(out=raw[:, 0:4], in_=tails[:, t, :]), reads=[("tails", t)], writes=[raw_k])
            A("pool", lambda e: e.tensor_copy(out=tails[:, t, :], in_=raw[:, T:T + 4]), reads=[raw_k], writes=[("tails", t)])
            return raw, raw_k, acc, acc_k

        def conv_tap(raw, raw_k, acc, acc_k, t, kk):
            sh = 1 + kk
            A("dve", lambda e: e.scalar_tensor_tensor(out=acc[:, 0:T], in0=raw[:, sh:sh + T], scalar=pp[:, PP_CW + 4 * t + kk:PP_CW + 4 * t + kk + 1],
                                                      in1=acc[:, 0:T], op0=ALU.mult, op1=ALU.add),
              reads=[raw_k, acc_k, "pp"], writes=[acc_k])

        def phase_A_jobs(blk):
            t0 = blk * T
            jobs = []
            for c in range(NCH):
                xa, xa_k = r4k.get()
                dma_in("sp", xa[:], x_d[t0 + c * 128:t0 + (c + 1) * 128, :], [xa_k], xa_k)
                jobs.append((c, xa[:], xa_k, 1, 0, hT, ("hT", c), False))
            return jobs

        def phase_AB(blk, pf):
            t0 = blk * T
            for j in range(2):
                W, W_k = pf.get(j)
                for f in range(4):
                    ft = 4 * j + f
                    b = alloc()
                    fm_matmuls(W, W_k, f, hT, hT_keys, b)
                    A("act", lambda e, b=b, ft=ft: e.activation(out=guT[:, ft, :], in_=ps[:, b, 0:T], func=AF.Gelu_apprx_tanh),
                      reads=pk(b), writes=[("guT", ft)])
                    yield 0.5
                pf.done(j)
            Wv = [pf.get(2), pf.get(3)]
            for c in range(NCH):
                b = alloc(2)
                for j in range(2):
                    for k in range(8):
                        A("pe", lambda e, k=k, j=j, c=c, b=b, Wj=Wv[j][0]: e.matmul(ps[:, b + j, :], lhsT=hT[:, k, c * 128:(c + 1) * 128], rhs=Wj[:, k, :],
                                                                                   start=(k == 0), stop=(k == 7)),
                          reads=[Wv[j][1], ("hT", c)], writes=pk(b + j))
                gv, gv_k = r4k.get()
                A("act", lambda e, b=b, gv=gv: e.activation(out=gv[:], in_=ps[:, b:b + 2, :].rearrange("p a n -> p (a n)"), func=AF.Gelu_apprx_tanh),
                  reads=pk(b, 2), writes=[gv_k])
                ss, ss_k = rst.get()
                A("act", lambda e, gv=gv, ss=ss: e.activation(out=jk2b, in_=gv[:], func=AF.Square, accum_out=ss[:, 0:1]),
                  reads=[gv_k], writes=["jk2", ss_k])
                rs, rs_k = rstd_from_ss(ss[:, 0:1], ss_k, D)
                A("dve", lambda e, gv=gv, rs=rs, c=c: e.scalar_tensor_tensor(out=vn[:, c, :], in0=gv[:], scalar=rs, in1=bp[:, BP_GMNW:BP_GMNW + D],
                                                                             op0=ALU.mult, op1=ALU.mult),
                  reads=[gv_k, rs_k, "bp"], writes=[("vn", c)])
                yield 1.0
            pf.done(2)
            pf.done(3)
            for c in range(NCH):
                b = alloc()
                for k in range(8):
                    A("pe", lambda e, k=k, c=c, b=b: e.matmul(ps[:, b, 0:32], lhsT=hT[:, k, c * 128:(c + 1) * 128], rhs=wdt[:, k, :],
                                                              start=(k == 0), stop=(k == 7)),
                      reads=["wdt", ("hT", c)], writes=pk(b))
                A("dve", lambda e, c=c, b=b: e.tensor_tensor(out=dtx[:, c, :], in0=ps[:, b, 0:32], in1=bp[:, BP_DTB:BP_DTB + 32], op=ALU.add),
                  reads=pk(b) + ["bp"], writes=["dtx"])
            dtx_f = dtx[:].rearrange("p c h -> p (c h)")
            dtt_f = dtt[:].rearrange("p c h -> p (c h)")
            dtl_f = dtl[:].rearrange("p c h -> p (c h)")
            A("act", lambda e: e.activation(out=dtt_f, in_=dtx_f, func=AF.Abs), reads=["dtx"], writes=["dtt"])
            A("act", lambda e: e.activation(out=dtl_f, in_=dtt_f, func=AF.Exp, scale=-1.0), reads=["dtt"], writes=["dtl"])
            A("act", lambda e: e.activation(out=dtl_f, in_=dtl_f, func=AF.Ln, bias=1.0), reads=["dtl"], writes=["dtl"])
            A("dve", lambda e: e.scalar_tensor_tensor(out=dtt_f, in0=dtx_f, scalar=0.0, in1=dtl_f, op0=ALU.max, op1=ALU.add),
              reads=["dtx", "dtl"], writes=["dtt"])
            A("dve", lambda e: e.tensor_tensor(out=a_all[:], in0=dtt[:], in1=negA[:].unsqueeze(1).broadcast_to([128, NCH, 32]), op=ALU.mult),
              reads=["dtt", "negA"], writes=["a_all"])
            A("dve", lambda e: e.tensor_copy(out=a_hi[:], in_=a_all[:]), reads=["a_all"], writes=["a_hi"])
            yield 0.5
            for j in range(8):
                W, W_k = pf.get(4 + j)
                for fp in range(2):
                    tl = [4 * j + 2 * fp, 4 * j + 2 * fp + 1]
                    fr = [conv_tile_front(W, W_k, 2 * fp + i, tl[i]) for i in range(2)]
                    for kk in (2, 1, 0):
                        for i in range(2):
                            conv_tap(fr[i][0], fr[i][1], fr[i][2], fr[i][3], tl[i], kk)
                    for i in range(2):
                        A("act", lambda e, acc=fr[i][2], t=tl[i]: e.activation(out=big[:, t, :], in_=acc[:, 0:T], func=AF.Silu),
                          reads=[fr[i][3]], writes=[("big", tl[i])])
                    yield 1.2
                pf.done(4 + j)
            for j in range(4):
                W, W_k = pf.get(12 + j)
                for c in range(NCH):
                    b = alloc()
                    for k in range(8):
                        A("pe", lambda e, k=k, c=c, b=b, W=W: e.matmul(ps[:, b, :], lhsT=hT[:, k, c * 128:(c + 1) * 128], rhs=W[:, k, :],
                                                                      start=(k == 0), stop=(k == 7)),
                          reads=[W_k, ("hT", c)], writes=pk(b))
                    A("act", lambda e, b=b, c=c, j=j: e.activation(out=sz[:, c, j * 512:(j + 1) * 512], in_=ps[:, b, :], func=AF.Silu),
                      reads=pk(b), writes=[("sz", c)])
                    yield 0.3
                pf.done(12 + j)

        def chunk_pre(c):
            csl = slice(c * 128, (c + 1) * 128)
            b = alloc(2)
            for g in range(8):
                bb = b + g // 4
                col = (g % 4) * 128
                A("pe", lambda e, g=g, bb=bb, col=col: e.matmul(ps[:, bb, col:col + 128], lhsT=vn[:, c, g * 128:(g + 1) * 128], rhs=wsTc[:, g, :],
                                                               start=(g % 4 == 0), stop=False, skip_group_check=True),
                  reads=[("vn", c), "wsTc"], writes=pk(bb))
                for hl in range(2):
                    A("pe", lambda e, g=g, bb=bb, col=col, hl=hl: e.matmul(ps[:, bb, col:col + 128], lhsT=cbf[0:8, 0, g:g + 1].broadcast_to([8, 128]), rhs=bsmat[:, hl, :],
                                                                          start=False, stop=(hl == 1), skip_group_check=True),
                      reads=["ident_bf", "bsH"], writes=pk(bb))
            A("dve", lambda e: e.tensor_tensor(out=y_aT[:, :, csl], in0=ps[:, b:b + 2, :].rearrange("p a (g i) -> p (a g) i", g=4),
                                               in1=guT[:, :, csl], op=ALU.mult),
              reads=pk(b, 2) + [("guT", ft) for ft in range(8)], writes=[("y_aT", c)])
            b2 = alloc()
            A("pe", lambda e: e.matmul(ps[:, b2, 0:32], lhsT=Umat, rhs=a_all[:, c, :], start=True, stop=True, skip_group_check=True),
              reads=["consts", "a_all"], writes=pk(b2))
            A("pe", lambda e: e.matmul(ps[:, b2, 32:64], lhsT=ones_f, rhs=a_all[:, c, :], start=False, stop=True, skip_group_check=True),
              reads=["consts", "a_all"], writes=pk(b2))
            cl, ecl, wend = cl_buf, ecl_buf, wend_buf
            A("dve", lambda e: e.tensor_copy(out=cl[:], in_=ps[:, b2, 0:64]), reads=pk(b2), writes=["cl"])
            A("act", lambda e: e.activation(out=ecl[:], in_=cl[:], func=AF.Exp), reads=["cl"], writes=["ecl"])
            A("dve", lambda e: e.tensor_tensor(out=wend[:, 0:32], in0=cl[:, 32:64], in1=cl[:, 0:32], op=ALU.subtract), reads=["cl"], writes=["wend"])
            A("act", lambda e: e.activation(out=wend[:, 32:64], in_=wend[:, 0:32], func=AF.Exp), reads=["wend"], writes=["wend"])
            b3 = alloc(2)
            for t in range(16):
                bb = b3 + t // 8
                psb = ps[:, bb, :].bitcast(BF16)
                A("pe", lambda e, t=t, psb=psb: e.transpose(out=psb[:, (t % 8) * 128:(t % 8 + 1) * 128], in_=big[:, t, csl], identity=ident_bf),
                  reads=[("big", t), "ident_bf"], writes=pk(bb))
            for hb in range(2):
                psb = ps[:, b3 + hb, :].bitcast(BF16)
                A("dve", lambda e, psb=psb, hb=hb: e.tensor_tensor(out=xs_tok[:, hb * 1024:(hb + 1) * 1024].rearrange("p (h d) -> p h d", h=16),
                                                                   in0=psb.rearrange("p (h d) -> p h d", h=16),
                                                                   in1=dtt[:, c, hb * 16:(hb + 1) * 16].unsqueeze(2).broadcast_to([128, 16, 64]), op=ALU.mult),
                  reads=pk(b3 + hb) + ["dtt"], writes=["xs_tok"])
                A("pool", lambda e, hb=hb: e.tensor_tensor(out=xw[:, hb * 1024:(hb + 1) * 1024].rearrange("p (h d) -> p h d", h=16),
                                                           in0=xs_tok[:, hb * 1024:(hb + 1) * 1024].rearrange("p (h d) -> p h d", h=16),
                                                           in1=wend[:, 32 + hb * 16:32 + (hb + 1) * 16].unsqueeze(2).broadcast_to([128, 16, 64]), op=ALU.mult),
                  reads=["xs_tok", "wend"], writes=["xw"])
            b4 = alloc()
            psb4 = ps[:, b4, :].bitcast(BF16)
            for g in range(8):
                A("pe", lambda e, g=g: e.transpose(out=psb4[:, g * 128:(g + 1) * 128], in_=big[:, 16 + g, csl], identity=ident_bf),
                  reads=[("big", 16 + g), "ident_bf"], writes=pk(b4))
            A("act", lambda e: e.activation(out=B_tok[:], in_=psb4, func=AF.Copy), reads=pk(b4), writes=["B_tok"])
            b5 = alloc(2)
            for g in range(8):
                bb = b5 + g // 4
                col = (g % 4) * 128
                A("pe", lambda e, g=g, bb=bb, col=col: e.matmul(ps[:, bb, col:col + 128], lhsT=big[:, 16 + g, csl], rhs=big[:, 24 + g, csl],
                                                               start=(g % 4 == 0), stop=True, skip_group_check=True),
                  reads=[("big", 16 + g), ("big", 24 + g)], writes=pk(bb))
            A("dve", lambda e: e.tensor_tensor(out=cbm[:].rearrange("p (g i) -> p g i", g=8),
                                               in0=ps[:, b5:b5 + 2, :].rearrange("p a (g i) -> p (a g) i", g=4),
                                               in1=Umat.unsqueeze(1).broadcast_to([128, 8, 128]), op=ALU.mult),
              reads=pk(b5, 2) + ["consts"], writes=["cbm"])

        def build_aU(c, q):
            A("dve", lambda e: e.tensor_tensor(out=aUh[:].rearrange("p (h i) -> p h i", h=8),
                                               in0=a_hi[:, c, q * 8:(q + 1) * 8].unsqueeze(2).broadcast_to([128, 8, 128]),
                                               in1=Ubf.unsqueeze(1).broadcast_to([128, 8, 128]), op=ALU.mult),
              reads=["a_hi", "ident_bf"], writes=["aUh"])

        def stA(c, g, ctx):
            gl = g % 2
            b = alloc()
            A("pe", lambda e: e.matmul(ps[:, b, :], lhsT=Vbf, rhs=aUh[:, gl * 512:(gl + 1) * 512], start=True, stop=True),
              reads=["ident_bf", "aUh"], writes=pk(b))
            E, E_k = rE.get()
            A("act", lambda e: e.activation(out=E[:], in_=ps[:, b, :], func=AF.Exp), reads=pk(b), writes=[E_k])
            AT, AT_k = rAT.get()
            ATv = AT[:].rearrange("p (h i) -> p h i", h=4)
            A("dve", lambda e: e.tensor_tensor(out=ATv, in0=E[:].rearrange("p (h i) -> p h i", h=4),
                                               in1=cbm[:, g * 128:(g + 1) * 128].unsqueeze(1).broadcast_to([128, 4, 128]), op=ALU.mult),
              reads=[E_k, "cbm"], writes=[AT_k])
            ctx["AT"] = (ATv, AT_k)

        def stB(c, g, ctx):
            csl = slice(c * 128, (c + 1) * 128)
            ATv, AT_k = ctx["AT"]
            ecl = ecl_buf
            by = alloc()
            for i in range(2):
                t = 2 * g + i
                A("pe", lambda e, i=i, t=t: e.matmul(ps[:, by, i * 128:(i + 1) * 128], lhsT=big[:, t, csl], rhs=diagD[:, t, :],
                                                     start=(i == 0), stop=False, skip_group_check=True),
                  reads=[("big", t), "diagD"], writes=pk(by))
            for hh in range(4):
                h = 4 * g + hh
                A("pe", lambda e, hh=hh, h=h: e.matmul(ps[:, by, hh * 64:(hh + 1) * 64], lhsT=ATv[:, hh, :], rhs=xs_tok[:, h * 64:(h + 1) * 64],
                                                       start=False, stop=(hh == 3), skip_group_check=True),
                  reads=[AT_k, "xs_tok"], writes=pk(by))
            A("pe", lambda e: e.matmul(ps[:, by, 256:512], lhsT=big[:, 24 + g, csl], rhs=Sbf[:, g * 256:(g + 1) * 256],
                                       start=False, stop=True, skip_group_check=True),
              reads=[("big", 24 + g), ("Sbf", g)], writes=pk(by))
            bs_ = alloc()
            A("pe", lambda e: e.matmul(ps[:, bs_, 0:256], lhsT=B_tok[:, g * 128:(g + 1) * 128], rhs=xw[:, g * 256:(g + 1) * 256], start=True, stop=True),
              reads=["B_tok", "xw"], writes=pk(bs_))
            yi, yi_k = ryi.get()
            A("dve", lambda e: e.tensor_tensor(out=yi[:].rearrange("p (h d) -> p h d", h=4), in0=ps[:, by, 256:512].rearrange("p (h d) -> p h d", h=4),
                                               in1=ecl[:, 4 * g:4 * g + 4].unsqueeze(2).broadcast_to([128, 4, 64]), op=ALU.mult),
              reads=pk(by) + ["ecl"], writes=[yi_k])
            A("dve", lambda e: e.tensor_tensor(out=yi[:], in0=ps[:, by, 0:256], in1=yi[:], op=ALU.add), reads=pk(by) + [yi_k], writes=[yi_k])
            A("dve", lambda e: e.tensor_tensor(out=yi[:], in0=yi[:], in1=sz[:, c, g * 256:(g + 1) * 256], op=ALU.mult),
              reads=[yi_k, ("sz", c)], writes=[yi_k])
            Sg = Sst[:, g * 256:(g + 1) * 256]
            A("dve", lambda e: e.tensor_tensor(out=Sg.rearrange("p (h d) -> p h d", h=4), in0=Sg.rearrange("p (h d) -> p h d", h=4),
                                               in1=ecl[:, 32 + 4 * g:32 + 4 * g + 4].unsqueeze(2).broadcast_to([128, 4, 64]), op=ALU.mult),
              reads=[("Sst", g), "ecl"], writes=[("Sst", g)])
            A("dve", lambda e: e.tensor_tensor(out=Sg, in0=ps[:, bs_, 0:256], in1=Sg, op=ALU.add), reads=pk(bs_) + [("Sst", g)], writes=[("Sst", g)])
            A("act", lambda e: e.activation(out=Sbf[:, g * 256:(g + 1) * 256], in_=Sg, func=AF.Copy), reads=[("Sst", g)], writes=[("Sbf", g)])
            ctx["yi"] = (yi, yi_k)

        def stC(c, g, ctx):
            yi, yi_k = ctx["yi"]
            sg, sg_k = rst.get()
            A("act", lambda e: e.activation(out=jk2[:, 0:256], in_=yi[:], func=AF.Square, accum_out=sg[:, 0:1]), reads=[yi_k], writes=["jk2", sg_k])
            ctx["rs"] = rstd_from_ss(sg[:, 0:1], sg_k, 256)

        def stC2(c, g, ctx):
            yi, yi_k = ctx["yi"]
            rs, rs_k = ctx["rs"]
            gs, gs_k = rgs.get()
            A("act", lambda e: e.activation(out=gs[:], in_=yi[:], func=AF.Identity, scale=rs), reads=[yi_k, rs_k], writes=[gs_k])
            ctx["gs"] = (gs, gs_k)

        def stD(c, g, ctx):
            csl = slice(c * 128, (c + 1) * 128)
            gs, gs_k = ctx["gs"]
            bb = alloc()
            psb = ps[:, bb, :].bitcast(BF16)
            for i in range(2):
                A("pe", lambda e, i=i: e.transpose(out=psb[:, i * 128:(i + 1) * 128], in_=gs[:, i * 128:(i + 1) * 128], identity=ident_bf),
                  reads=[gs_k, "ident_bf"], writes=pk(bb))
            for i in range(2):
                t = 2 * g + i
                A("act", lambda e, t=t, i=i: e.activation(out=y_bT[:, t, csl], in_=psb[:, i * 128:(i + 1) * 128], func=AF.Identity,
                                                          scale=pp[:, PP_NW + t:PP_NW + t + 1]),
                  reads=pk(bb) + ["pp"], writes=[("y_bT", c)])

        def phase_C(blk):
            for c in range(NCH):
                chunk_pre(c)
                yield 2.0
                ctxs = [dict() for _ in range(8)]
                build_aU(c, 0)
                for step in range(8 + 3):
                    if 0 <= step - 2 < 8:
                        stC(c, step - 2, ctxs[step - 2])
                    if 0 <= step - 1 < 8:
                        stB(c, step - 1, ctxs[step - 1])
                    if step < 8:
                        stA(c, step, ctxs[step])
                        if step in (1, 3, 5):
                            build_aU(c, (step + 1) // 2)
                    yield 1.2
                    for _ in range(NFILL):
                        A("pe", lambda e: e.matmul(ps[:, 4, :], lhsT=ident_bf, rhs=wsTc[:, 0:4, :].rearrange("p g i -> p (g i)"), start=True, stop=True),
                          reads=["wsTc", "ident_bf"])
                    if 0 <= step - 2 < 8:
                        stC2(c, step - 2, ctxs[step - 2])
                    if 0 <= step - 3 < 8:
                        stD(c, step - 3, ctxs[step - 3])

        y_aT_keys = [("y_aT", c) for c in range(NCH)]
        y_bT_keys = [("y_bT", c) for c in range(NCH)]

        def phase_D(blk, pf, hook=None):
            if hook is not None:
                hook(0)
            for j in range(2):
                W, W_k = pf.get(16 + 5 * j)
                sa = []
                for f in range(4):
                    b = alloc()
                    fm_matmuls(W, W_k, f, hT, hT_keys, b)
                    s_, s_k = rg.get()
                    A("act", lambda e, b=b, s_=s_: e.activation(out=s_[:, 0:T], in_=ps[:, b, 0:T], func=AF.Sigmoid), reads=pk(b), writes=[s_k])
                    sa.append((s_, s_k))
                    yield 0.1
                pf.done(16 + 5 * j)
                W, W_k = pf.get(17 + 5 * j)
                for f in range(4):
                    b = alloc()
                    fm_matmuls(W, W_k, f, y_aT, y_aT_keys, b)
                    s_, s_k = sa[f]
                    A("dve", lambda e, b=b, s_=s_: e.tensor_tensor(out=s_[:, 0:T], in0=ps[:, b, 0:T], in1=s_[:, 0:T], op=ALU.mult),
                      reads=pk(b) + [s_k], writes=[s_k])
                    yield 0.1
                pf.done(17 + 5 * j)
                W, W_k = pf.get(18 + 5 * j)
                sbb = []
                for f in range(4):
                    b = alloc()
                    fm_matmuls(W, W_k, f, hT, hT_keys, b)
                    s_, s_k = rg.get()
                    A("act", lambda e, b=b, s_=s_: e.activation(out=s_[:, 0:T], in_=ps[:, b, 0:T], func=AF.Sigmoid), reads=pk(b), writes=[s_k])
                    sbb.append((s_, s_k))
                    yield 0.1
                pf.done(18 + 5 * j)
                if j == 1 and hook is not None:
                    hook(1)
                W0, W0_k = pf.get(19 + 5 * j)
                W1, W1_k = pf.get(20 + 5 * j)
                for f in range(4):
                    b = alloc()
                    for kk in range(16):
                        Wx, Wx_k = (W0, W0_k) if kk < 8 else (W1, W1_k)
                        A("pe", lambda e, kk=kk, Wx=Wx, f=f, b=b: e.matmul(ps[:, b, 0:T], lhsT=Wx[:, kk % 8, f * 128:(f + 1) * 128], rhs=y_bT[:, kk, :],
                                                                          start=(kk == 0), stop=(kk == 15)),
                          reads=[Wx_k] + y_bT_keys, writes=pk(b))
                    s_, s_k = sbb[f]
                    A("dve", lambda e, b=b, s_=s_: e.tensor_tensor(out=s_[:, 0:T], in0=ps[:, b, 0:T], in1=s_[:, 0:T], op=ALU.mult),
                      reads=pk(b) + [s_k], writes=[s_k])
                    a_, a_k = sa[f]
                    ft = 4 * j + f
                    A("dve", lambda e, s_=s_, a_=a_, ft=ft: e.tensor_tensor(out=guT[:, ft, :], in0=a_[:, 0:T], in1=s_[:, 0:T], op=ALU.add),
                      reads=[s_k, a_k], writes=[("guT", ft)])
                    yield 0.1
                pf.done(19 + 5 * j)
                pf.done(20 + 5 * j)
            if hook is not None:
                hook(2)

        mixT = guT
        mix_keys = [("guT", ft) for ft in range(8)]

        h2T_keys = [("h2T", c) for c in range(NCH)]

        def x_reread(blk):
            t0 = blk * T
            for c in range(NCH):
                dma_in("pool", xres[:, c, :], x_d[t0 + c * 128:t0 + (c + 1) * 128, :], [("xres", c)], ("xres", c))

        def phase_E_mm(blk, Wpre=None, x_done=False):
            if not x_done:
                x_reread(blk)
            for n in range(2):
                W, W_k = Wpre[n] if Wpre is not None else wload(SLOTS["out"][0] + n)
                for c in range(NCH):
                    b = alloc() if (n == 1 and c == NCH - 1) else alloc_y()
                    for k in range(8):
                        A("pe", lambda e, k=k, c=c, b=b, W=W: e.matmul(ps[:, b, :], lhsT=mixT[:, k, c * 128:(c + 1) * 128], rhs=W[:, k, :],
                                                                      start=(k == 0), stop=(k == 7)),
                          reads=[W_k] + mix_keys, writes=pk(b))
                    A("dve", lambda e, b=b, c=c, n=n: e.tensor_tensor(out=xres[:, c, n * 512:(n + 1) * 512], in0=ps[:, b, :], in1=xres[:, c, n * 512:(n + 1) * 512], op=ALU.add),
                      reads=pk(b) + [("xres", c)], writes=[("xres", c)])
                wring.release(W_k)
            return [(c, xres[:, c, :], ("xres", c), 3, 2, h2T, ("h2T", c), True) for c in range(NCH)]

        def phase_F(blk):
            for j in range(8):
                W, W_k = wload(SLOTS["ff1"][0] + j)
                for f in range(4):
                    ft = 4 * j + f
                    b = alloc_y()
                    fm_matmuls(W, W_k, f, h2T, h2T_keys, b)
                    r_, r_k = rT_y.get()
                    A("act", lambda e, b=b, r_=r_: e.activation(out=r_[:, 0:T], in_=ps[:, b, 0:T], func=AF.Relu), reads=pk(b), writes=[r_k])
                    A("act", lambda e, r_=r_, ft=ft: e.activation(out=fT[:, ft, :], in_=r_[:, 0:T], func=AF.Square),
                      reads=[r_k], writes=[("fT", ft)])
                    yield 1.0
                wring.release(W_k)

        def phase_G(blk):
            t0 = blk * T
            for n in range(2):
                b0 = alloc_y(NCH)
                for kg in range(4):
                    W, W_k = wload(SLOTS["ff2"][0] + 4 * n + kg)
                    for c in range(NCH):
                        for k in range(8):
                            kt = kg * 8 + k
                            A("pe", lambda e, k=k, kt=kt, c=c, W=W, b0=b0, kg=kg: e.matmul(ps[:, b0 + c, :], lhsT=fT[:, kt, c * 128:(c + 1) * 128], rhs=W[:, k, :],
                                                                                         start=(kg == 0 and k == 0), stop=(kg == 3 and k == 7)),
                              reads=[W_k, ("fT", kt)], writes=pk(b0 + c))
                        yield 1.9
                    wring.release(W_k)
                for c in range(NCH):
                    A("dve", lambda e, c=c, n=n, b0=b0: e.tensor_tensor(out=xres[:, c, n * 512:(n + 1) * 512], in0=ps[:, b0 + c, :], in1=xres[:, c, n * 512:(n + 1) * 512], op=ALU.add),
                      reads=pk(b0 + c) + [("xres", c)], writes=[("xres", c)])
            for c in range(NCH):
                ss, ss_k = rst_y.get()
                A("act", lambda e, ss=ss, c=c: e.activation(out=jk2b, in_=xres[:, c, :], func=AF.Square, accum_out=ss[:, 0:1]),
                  reads=[("xres", c)], writes=["jk2", ss_k])
                rs, rs_k = rstd_from_ss(ss[:, 0:1], ss_k, D, rst_y)
                A("dve", lambda e, rs=rs, c=c: e.scalar_tensor_tensor(out=xres[:, c, :], in0=xres[:, c, :], scalar=rs, in1=bp[:, BP_FNW:BP_FNW + D], op0=ALU.mult, op1=ALU.mult),
                  reads=[("xres", c), rs_k, "bp"], writes=[("xres", c)])
                A("pool", lambda e, c=c, t0=t0: e.dma_start(out=out_d[t0 + c * 128:t0 + (c + 1) * 128, :], in_=xres[:, c, :]),
                  reads=[("xres", c)], dma=True, semkey=("o", c))
                yield 1.0

        def run_all(gen):
            for _ in gen:
                pass

        def chainX(blk, pf, hook=None):
            yield from phase_AB(blk, pf)
            yield from phase_C(blk)
            yield from phase_D(blk, pf, hook)

        def chainY(blk, jobsE=()):
            for jb in jobsE:
                norm_multi([jb])
                yield 1.0
            yield from phase_F(blk)
            yield from phase_G(blk)

        W_TOTAL = 8 * 0.5 + 2 * 1.0 + 0.5 + 16 * 1.2 + 8 * 0.3 + NCH * (2.0 + 11 * 1.2) + 32 * 0.1
        Y_TOTAL = 32 * 1.0 + 16 * 1.9 + 2 * 1.0

        ystate = {"done": True}

        def interleave(X, Y):
            acc = 0.0
            ynext = next(Y, None) if Y is not None else None
            ystate["done"] = ynext is None
            if X is not None:
                for w in X:
                    acc += w * (Y_TOTAL / W_TOTAL)
                    while ynext is not None and acc >= ynext:
                        acc -= ynext
                        ynext = next(Y, None)
                    ystate["done"] = ynext is None
            while ynext is not None:
                ynext = next(Y, None)
            ystate["done"] = True

        if OVERLAP:
            early = {}

            def make_hook(blk):
                def hook(stage):
                    nxt = blk + 1 < NBLK
                    if stage == 0 and nxt:
                        early["jobsA"] = phase_A_jobs(blk + 1)
                    elif stage == 1:
                        early["Wout"] = [wload(SLOTS["out"][0] + n) for n in range(2)]
                        if nxt:
                            early["stA"] = norm_head(early["jobsA"])
                    elif stage == 2:
                        if nxt:
                            norm_tail(early.pop("jobsA"), early.pop("stA"))
                        if ystate["done"]:
                            x_reread(blk)
                            early["xre"] = True
                return hook

            norm_multi(phase_A_jobs(0))
            for blk in range(NBLK + 1):
                pf = Prefetch()
                jobsE = []
                if blk < NBLK:
                    pf.get(0)
                if blk >= 1:
                    jobsE = phase_E_mm(blk - 1, early.pop("Wout"), early.pop("xre", False))
                interleave(chainX(blk, pf, make_hook(blk)) if blk < NBLK else None, chainY(blk - 1, jobsE) if blk >= 1 else None)
        else:
            for blk in range(NBLK):
                norm_multi(phase_A_jobs(blk))
                run_all(chainX(blk, Prefetch()))
                for jb in phase_E_mm(blk):
                    norm_multi([jb])
                run_all(chainY(blk))

        A("sp", None, writes=[("xres", c) for c in range(NCH)])
        if DEBUG_DUMP:
            lastd = [o for o in S_.ops if o.dma and isinstance(o.semkey, tuple) and o.semkey[0] == "dbg"]
            op = S_.add("sp", None)
            op.deps = [(p, True) for p in lastd]
        S_.emit(nc)
    print("ops:", {e: len(S_.streams[e]) for e in ENG_NAMES}, "sems:", S_.n_sems)
    return nc


def _host_layout(inputs, b):
    f = np.float32
    c = np.asarray(inputs["c"], f)[b]
    conv_w = np.asarray(inputs["conv_w"], f)[0]
    conv_b = np.asarray(inputs["conv_b"], f)[0]
    ssm_nw = np.asarray(inputs["ssm_norm_w"], f)[0]
    d_skip = np.asarray(inputs["d_skip"], f)[0]
    pp = np.zeros((128, PP_N), f)
    pp[:, PP_C:PP_C + 8] = c.reshape(8, 128).T
    cw = conv_w.reshape(4, 32, 128)
    pp[:, PP_CW:PP_CW + 128] = cw.transpose(2, 1, 0).reshape(128, 128)
    pp[:, PP_CB:PP_CB + 32] = conv_b.reshape(32, 128).T
    pp[:, PP_NW:PP_NW + 16] = ssm_nw.reshape(16, 128).T
    ch = np.arange(2048).reshape(16, 128).T
    pp[:, PP_DS:PP_DS + 16] = d_skip[ch // 64]
    bp = np.zeros((128, BP_N), f)
    bp[:, BP_DTB:BP_DTB + 32] = np.asarray(inputs["dt_bias"], f)[0][None, :]
    bp[:, BP_ALOG:BP_ALOG + 32] = np.asarray(inputs["a_log"], f)[0][None, :]
    bp[:, BP_GMNW:BP_GMNW + D] = np.asarray(inputs["gm_norm_w"], f)[0][None, :]
    bp[:, BP_FNW:BP_FNW + D] = np.asarray(inputs["final_norm_w"], f)[None, :]
    return pp, bp


def _consts():
    k = np.arange(128)
    ident = np.eye(128, dtype=np.float32)
    U = (k[:, None] <= k[None, :]).astype(np.float32)
    V = (k[:, None] > k[None, :]).astype(np.float32)
    ones = np.ones((128, 128), np.float32)
    return np.ascontiguousarray(np.stack([ident, U, V, ones], axis=1))


def run(inputs, S=None, T=256, cores=None, trace=False):
    f = np.float32
    x = np.asarray(inputs["x"], f)
    B = x.shape[0]
    if S is None:
        S = x.shape[1]
    cores = list(range(B)) if cores is None else cores
    nc = build_nc(S, T)
    consts = _consts()
    shared = {
        "w_mod": np.ascontiguousarray(np.asarray(inputs["w_mod"], f)[0]),
        "bmod": np.ascontiguousarray(np.broadcast_to(np.asarray(inputs["b_mod"], f)[0][None, :], (128, 6 * D))),
        "w_in": np.ascontiguousarray(np.asarray(inputs["w_in"], f)[0]),
        "w_gm": np.ascontiguousarray(np.asarray(inputs["w_branch_gm"], f)[0]),
        "w_ssm": np.ascontiguousarray(np.asarray(inputs["w_branch_ssm"], f)[0]),
        "w_out": np.ascontiguousarray(np.asarray(inputs["w_out"], f)[0]),
        "w_ff1": np.ascontiguousarray(np.asarray(inputs["w_ff1"], f)[0]),
        "w_ff2": np.ascontiguousarray(np.asarray(inputs["w_ff2"], f)[0]),
        "consts": consts,
        "bsrow": np.ascontiguousarray(np.asarray(inputs["gm_bs"], f)[0].reshape(8, 128)),
        "wsT": np.ascontiguousarray(np.asarray(inputs["gm_ws"], f)[0].transpose(2, 0, 1)),
    }
    in_maps = []
    for b in cores:
        pp, bp = _host_layout(inputs, b)
        m = dict(shared)
        m["x"] = np.ascontiguousarray(x[b, :S])
        m["pp"] = pp
        m["bp"] = bp
        in_maps.append(m)
    res = run_bass_kernel_spmd(nc, in_maps, core_ids=list(range(len(cores))), trace=trace)
    out = np.stack([np.asarray(r["out"], dtype=f) for r in res.results], axis=0)
    return out, res


def kernel(**inputs):
    out, _ = run(inputs)
    return out
```

```python
import contextlib
import numpy as np
import concourse.bass as bass
import concourse.mybir as mybir
from concourse.bass_utils import run_bass_kernel_spmd

F32 = mybir.dt.float32
BF16 = mybir.dt.bfloat16
AF = mybir.ActivationFunctionType
ALU = mybir.AluOpType
AX = mybir.AxisListType

D = 1024
NKT = 8
DIN = 2048
NH = 32
NG = 8
DFF = 4096
EPS = 1e-6
OFF_U, OFF_V, OFF_Z, OFF_XBC, OFF_DT, OFF_GA, OFF_GB = 0, 1024, 2048, 4096, 8192, 8224, 9248
N_CORES = 8
DEBUG_BARRIER = False
DEBUG_DUMP = False
OVERLAP = True
NFILL = 0

ENG_NAMES = ("pe", "act", "dve", "pool", "sp")
EPOCH = 12000


class Op:
    __slots__ = ("eng", "fn", "reads", "writes", "dma", "semkey", "idx", "deps", "sig", "cnt", "dcount", "waits")

    def __init__(self, eng, fn, reads, writes, dma, semkey):
        self.eng = eng
        self.fn = fn
        self.reads = reads
        self.writes = writes
        self.dma = dma
        self.semkey = semkey
        self.deps = []
        self.sig = False
        self.cnt = 0
        self.dcount = 0
        self.waits = []


class Sched:
    def __init__(self):
        self.ops = []
        self.streams = {e: [] for e in ENG_NAMES}
        self.last_write = {}
        self.readers = {}
        self.dma_counts = {}

    def add(self, eng, fn, reads=(), writes=(), dma=False, semkey=None):
        op = Op(eng, fn, tuple(reads), tuple(writes), dma, semkey)
        if dma:
            assert semkey is not None
            n = self.dma_counts.get(semkey, 0) + 1
            self.dma_counts[semkey] = n
            op.dcount = n
        op.idx = len(self.streams[eng])
        self.streams[eng].append(op)
        self.ops.append(op)
        deps = {}
        for r in op.reads:
            w = self.last_write.get(r)
            if w is not None:
                deps[id(w)] = (w, True)
        for r in op.writes:
            w = self.last_write.get(r)
            if w is not None and id(w) not in deps:
                deps[id(w)] = (w, False)
            for rd in self.readers.get(r, ()):
                if id(rd) not in deps:
                    deps[id(rd)] = (rd, False)
        op.deps = list(deps.values())
        for r in op.reads:
            lst = self.readers.setdefault(r, [])
            if not dma:
                lst[:] = [o for o in lst if o.dma or o.eng != eng]
            lst.append(op)
        for r in op.writes:
            self.last_write[r] = op
            self.readers[r] = []
        return op

    def barrier(self):
        lasts = []
        for e in ENG_NAMES:
            comp = [o for o in self.streams[e] if not o.dma and o.fn is not None]
            if comp:
                lasts.append((comp[-1], True))
        lastd = {}
        for o in self.ops:
            if o.dma:
                lastd[o.semkey] = o
        lasts += [(o, True) for o in lastd.values()]
        for e in ENG_NAMES:
            op = Op(e, None, (), (), False, None)
            op.idx = len(self.streams[e])
            self.streams[e].append(op)
            self.ops.append(op)
            op.deps = [(p, True) for (p, _) in lasts]

    def finalize(self):
        seen = {e: {} for e in ENG_NAMES}
        for op in self.ops:
            need = {}
            for (p, raw) in op.deps:
                if p.dma:
                    k = ("d", p.semkey)
                    v = p.dcount
                else:
                    if p.eng == op.eng and not op.dma and op.fn is not None:
                        if p.eng == "pe":
                            continue
                        if not raw:
                            continue
                    k = ("e", p.eng)
                    v = p.idx + 1
                if seen[op.eng].get(k, 0) >= v:
                    continue
                if need.get(k, (0, None))[0] < v:
                    need[k] = (v, p)
            for k, (v, p) in need.items():
                seen[op.eng][k] = v
                p.sig = True
            op.waits = list(need.items())
        self.nsig = {}
        for e in ENG_NAMES:
            c = 0
            for op in self.streams[e]:
                if op.dma:
                    continue
                if op.sig:
                    c += 1
                    op.cnt = c
            self.nsig[e] = c

    def emit(self, nc):
        self.finalize()
        with contextlib.ExitStack() as st:
            esems = {}
            for e in ENG_NAMES:
                n = self.nsig[e] // EPOCH + 1
                esems[e] = [st.enter_context(nc.semaphore(f"s_{e}_{i}")) for i in range(n)]
            dsems = {}
            for k in self.dma_counts:
                dsems[k] = st.enter_context(nc.semaphore("d_" + str(len(dsems))))
            self.n_sems = sum(len(v) for v in esems.values()) + len(dsems)
            block = st.enter_context(nc.Block())

            def run_stream(ename):
                def body(eng):
                    for op in self.streams[ename]:
                        for k, (v, p) in op.waits:
                            if k[0] == "d":
                                eng.wait_ge(dsems[k[1]], 16 * p.dcount)
                            else:
                                c = p.cnt
                                ep = (c - 1) // EPOCH
                                eng.wait_ge(esems[k[1]][ep], c - ep * EPOCH)
                        if op.fn is None:
                            continue
                        ins = op.fn(eng)
                        if op.dma:
                            ins.then_inc(dsems[op.semkey], 16)
                        elif op.sig:
                            ep = (op.cnt - 1) // EPOCH
                            ins.then_inc(esems[ename][ep], 1)
                return body

            block.tensor(run_stream("pe"))
            block.scalar(run_stream("act"))
            block.vector(run_stream("dve"))
            block.gpsimd(run_stream("pool"))
            block.sync(run_stream("sp"))


class Ring:
    def __init__(self, name, tensors):
        self.name = name
        self.tensors = tensors
        self.i = 0

    def get(self):
        i = self.i
        self.i = (i + 1) % len(self.tensors)
        return self.tensors[i], (self.name, i)


class WRing:
    def __init__(self, name, tensors):
        self.name = name
        self.tensors = tensors
        self.free = list(range(len(tensors)))

    def get(self):
        assert self.free, "weight ring exhausted"
        i = self.free.pop(0)
        return self.tensors[i], (self.name, i)

    def release(self, key):
        assert key[1] not in self.free
        self.free.append(key[1])


SLOTS = {}
_n = 0
for _nm, _c in [("u", 2), ("xbc", 8), ("v", 2), ("z", 4), ("ga", 2), ("gb", 2), ("gm", 2), ("ssm", 4),
                ("out", 2), ("ff1", 8), ("ff2", 8)]:
    SLOTS[_nm] = (_n, _c)
    _n += _c
NSLOT = _n

PP_C, PP_CW, PP_CB, PP_NW, PP_DS, PP_N = 0, 8, 136, 168, 184, 200
BP_DTB, BP_ALOG, BP_GMNW, BP_FNW, BP_N = 0, 32, 64, 1088, 2112


def build_nc(S, T):
    NCH = T // 128
    assert T == 256
    NBLK = S // T
    assert S % T == 0 and T % 128 == 0
    nc = bass.Bass("TRN2", target_bir_lowering=False)

    def din(name, shape, dt=F32):
        return nc.dram_tensor(name, shape, dt, kind="ExternalInput").ap()

    x_d = din("x", [S, D])
    wmod_d = din("w_mod", [D, 6 * D])
    bmod_d = din("bmod", [128, 6 * D])
    win_d = din("w_in", [D, 10272])
    wgm_d = din("w_gm", [D, D])
    wssm_d = din("w_ssm", [DIN, D])
    wout_d = din("w_out", [D, D])
    wff1_d = din("w_ff1", [D, DFF])
    wff2_d = din("w_ff2", [DFF, D])
    consts_d = din("consts", [128, 4, 128])
    pp_d = din("pp", [128, PP_N])
    bp_d = din("bp", [128, BP_N])
    bsrow_d = din("bsrow", [8, 128])
    wsT_d = din("wsT", [128, 8, 128])
    out_d = nc.dram_tensor("out", [S, D], F32, kind="ExternalOutput").ap()
    wsl_d = nc.dram_tensor("wsl", [NSLOT, 128, 4096], BF16, kind="Internal").ap()
    wdt_d = nc.dram_tensor("wdt_s", [128, 256], BF16, kind="Internal").ap()

    S_ = Sched()
    A = S_.add
    dbg = {}

    def dump(name, ap, keys, shape, dt=F32):
        if not DEBUG_DUMP or name in dbg:
            return
        dbg[name] = nc.dram_tensor("dbg_" + name, list(shape), dt, kind="ExternalOutput").ap()
        A("sp", lambda e: e.dma_start(out=dbg[name], in_=ap), reads=keys, dma=True, semkey=("dbg", name))

    with contextlib.ExitStack() as st:
        def sb(name, shape, dt=F32):
            return st.enter_context(nc.sbuf_tensor("s_" + name, shape, dt))

        consts = sb("consts", [128, 4, 128])
        ident_f = consts[:, 0, :]
        Umat = consts[:, 1, :]
        Vmat = consts[:, 2, :]
        ones_f = consts[:, 3, :]
        cbf = sb("cbf", [128, 4, 128], BF16)
        ident_bf = cbf[:, 0, :]
        Ubf = cbf[:, 1, :]
        Vbf = cbf[:, 2, :]
        ones_bf = cbf[:, 3, :]
        pp = sb("pp", [128, PP_N])
        bp = sb("bp", [128, BP_N])
        bsmat = sb("bsmat", [8, 2, 128], BF16)
        wsTc = sb("wsTc", [128, 8, 128], BF16)
        diagD = sb("diagD", [128, 16, 128], BF16)
        wdt = sb("wdt", [128, 8, 32], BF16)
        negA = sb("negA", [128, 32])
        cact = sb("cact", [128, 8])
        modp = sb("modp", [128, 4, 8])
        mhalf = sb("mhalf", [128, 1])
        epsc = sb("epsc", [128, 1])
        Sst = sb("Sst", [128, 2048])
        Sbf = sb("Sbf", [128, 2048], BF16)
        tails = sb("tails", [128, 32, 4])
        xres = sb("xres", [128, NCH, D])
        hT = sb("hT", [128, NKT, T], BF16)
        h2T = sb("h2T", [128, NKT, T], BF16)
        fT = sb("fT", [128, 32, T], BF16)
        guT = sb("guT", [128, NKT, T], BF16)
        big = sb("big", [128, 32, T], BF16)
        vn = sb("vn", [128, NCH, D], BF16)
        sz = sb("sz", [128, NCH, DIN], BF16)
        dtx = sb("dtx", [128, NCH, 32])
        dtt = sb("dtt", [128, NCH, 32])
        dtl = sb("dtl", [128, NCH, 32])
        a_all = sb("a_all", [128, NCH, 32])
        y_aT = sb("y_aT", [128, NKT, T], BF16)
        y_bT = sb("y_bT", [128, 16, T], BF16)
        xs_tok = sb("xs_tok", [128, DIN], BF16)
        B_tok = sb("B_tok", [128, 1024], BF16)
        xw = sb("xw", [128, DIN], BF16)
        cl_buf = sb("cl_buf", [128, 64])
        ecl_buf = sb("ecl_buf", [128, 64])
        wend_buf = sb("wend_buf", [128, 64])
        a_hi = sb("a_hi", [128, NCH, 32], BF16)
        aUh = sb("aUh", [128, 1024], BF16)
        cbm = sb("cbm", [128, 1024], BF16)
        jk2 = sb("jk2", [128, 512])
        r2k = Ring("r2k", [sb(f"r2k{i}", [128, 512]) for i in range(2)])
        r2k_y = Ring("r2ky", [sb(f"r2ky{i}", [128, 512]) for i in range(1)])
        rg = Ring("rg", [sb(f"rg{i}", [128, 256]) for i in range(8)])
        rT_y = Ring("rTy", [sb(f"rTy{i}", [128, 256]) for i in range(2)])
        rE = Ring("rE", [sb(f"rE{i}", [128, 512]) for i in range(2)])
        rAT = Ring("rAT", [sb(f"rAT{i}", [128, 512], BF16) for i in range(3)])
        ryi = Ring("ryi", [sb(f"ryi{i}", [128, 256]) for i in range(3)])
        rgs = Ring("rgs", [sb(f"rgs{i}", [128, 256], BF16) for i in range(3)])
        r4k = Ring("r4k", [sb(f"r4k{i}", [128, 1024]) for i in range(2)])
        rraw = Ring("raw", [sb(f"raw{i}", [128, T + 4]) for i in range(3)])
        rst = Ring("st", [sb(f"st{i}", [128, 64]) for i in range(8)])
        rst_y = Ring("sty", [sb(f"sty{i}", [128, 64]) for i in range(4)])
        wring = WRing("wr", [sb(f"wr{i}", [128, 8, 512], BF16) for i in range(5)])
        ps = st.enter_context(nc.psum_tensor("ps", [128, 8, 512], F32))
        print("sbuf bytes remaining:", nc.sbuf_bytes_remaining)

        bank_ptr = [0]
        NXB = 5

        def alloc(n=1):
            b = bank_ptr[0]
            if b + n > NXB:
                b = 0
            bank_ptr[0] = (b + n) % NXB
            return b

        bank_ptr_y = [0]

        def alloc_y(n=1):
            b = bank_ptr_y[0]
            if b + n > 3:
                b = 0
            bank_ptr_y[0] = (b + n) % 3
            return 5 + b

        def pk(b, n=1):
            return [("ps", b + i) for i in range(n)]

        def dma_in(eng, out_ap, in_ap, wkeys, semkey, rkeys=()):
            A(eng, lambda e: e.dma_start(out=out_ap, in_=in_ap), reads=rkeys, writes=wkeys, dma=True, semkey=semkey)

        dma_in("sp", consts[:], consts_d, ["consts"], "c_consts")
        dma_in("sp", pp[:], pp_d, ["pp"], "c_pp")
        dma_in("sp", bp[:], bp_d, ["bp"], "c_bp")
        bs_st, bs_k = r4k.get()
        dma_in("sp", bs_st[0:8, 0:128], bsrow_d, [bs_k], "c_bsrow")
        wsT_st, wsT_k = r4k.get()
        dma_in("sp", wsT_st[:].rearrange("p (g i) -> p g i", g=8), wsT_d, [wsT_k], "c_wsT")

        def cast_slot(slot, W, r0, c0):
            src = W[r0:r0 + 1024, c0:c0 + 512].rearrange("(k p) n -> p k n", p=128)
            dst = wsl_d[slot].rearrange("p (k n) -> p k n", k=8)
            A("pool", lambda e: e.dma_start(out=dst, in_=src), writes=[("wsl", slot)], dma=True, semkey=("wsl", slot))

        for j in range(2):
            cast_slot(SLOTS["u"][0] + j, win_d, 0, OFF_U + 512 * j)
        for j in range(8):
            cast_slot(SLOTS["xbc"][0] + j, win_d, 0, OFF_XBC + 512 * j)
        for j in range(2):
            cast_slot(SLOTS["v"][0] + j, win_d, 0, OFF_V + 512 * j)
        for j in range(4):
            cast_slot(SLOTS["z"][0] + j, win_d, 0, OFF_Z + 512 * j)
        A("pool", lambda e: e.dma_start(out=wdt_d.rearrange("p (k n) -> p k n", k=8),
                                        in_=win_d[:, OFF_DT:OFF_DT + 32].rearrange("(k p) n -> p k n", p=128)),
          writes=["wdt_d"], dma=True, semkey="wdt_d")
        dma_in("sp", wdt[:], wdt_d.rearrange("p (k n) -> p k n", k=8), ["wdt"], "c_wdt", rkeys=["wdt_d"])
        for j in range(2):
            cast_slot(SLOTS["ga"][0] + j, win_d, 0, OFF_GA + 512 * j)
        for j in range(2):
            cast_slot(SLOTS["gm"][0] + j, wgm_d, 0, 512 * j)
        for j in range(2):
            cast_slot(SLOTS["gb"][0] + j, win_d, 0, OFF_GB + 512 * j)
        for j in range(2):
            for kh in range(2):
                cast_slot(SLOTS["ssm"][0] + 2 * j + kh, wssm_d, 1024 * kh, 512 * j)
        for j in range(8):
            cast_slot(SLOTS["ff1"][0] + j, wff1_d, 0, 512 * j)

        A("dve", lambda e: e.tensor_copy(out=cbf[:], in_=consts[:]), reads=["consts"], writes=["ident_bf"])
        A("dve", lambda e: e.tensor_copy(out=bsmat[:, 0, :], in_=bs_st[0:8, 0:128]), reads=[bs_k], writes=["bsH"])
        A("dve", lambda e: e.tensor_tensor(out=bsmat[:, 1, :], in0=bs_st[0:8, 0:128], in1=bsmat[:, 0, :], op=ALU.subtract),
          reads=[bs_k, "bsH"], writes=["bsH"])
        A("dve", lambda e: e.tensor_tensor(out=wsTc[:], in0=wsT_st[:].rearrange("p (g i) -> p g i", g=8),
                                           in1=Umat.unsqueeze(1).broadcast_to([128, 8, 128]), op=ALU.mult),
          reads=[wsT_k, "consts"], writes=["wsTc"])
        A("dve", lambda e: e.tensor_tensor(out=diagD[:], in0=ident_f.unsqueeze(1).broadcast_to([128, 16, 128]),
                                           in1=pp[:, PP_DS:PP_DS + 16].unsqueeze(2).broadcast_to([128, 16, 128]), op=ALU.mult),
          reads=["consts", "pp"], writes=["diagD"])
        A("act", lambda e: e.activation(out=negA[:], in_=bp[:, BP_ALOG:BP_ALOG + 32], func=AF.Exp), reads=["bp"], writes=["negA"])
        A("dve", lambda e: e.tensor_scalar(out=negA[:], in0=negA[:], scalar1=-1.0, scalar2=None, op0=ALU.mult),
          reads=["negA"], writes=["negA"])
        A("act", lambda e: e.activation(out=cact[:], in_=pp[:, PP_C:PP_C + 8], func=AF.Silu), reads=["pp"], writes=["cact"])
        A("pool", lambda e: e.memset(mhalf[:], -0.5), writes=["mhalf"])
        A("pool", lambda e: e.memset(epsc[:], EPS), writes=["mhalf"])
        A("pool", lambda e: e.memset(Sst[:], 0.0), writes=["Sst"])
        A("pool", lambda e: e.memset(Sbf[:], 0.0), writes=["Sbf"])
        A("pool", lambda e: e.memset(tails[:], 0.0), writes=["tails"])

        big_f = big[:].rearrange("p a b -> p (a b)").bitcast(F32)
        nstage = (16 * T) // 4096
        assert nstage >= 1
        stage_keys = []
        tiles_per_stage = 32 // nstage
        for i in range(nstage):
            stage_keys.append([("big", t) for t in range(i * tiles_per_stage, (i + 1) * tiles_per_stage)])
        stage_i = [0]

        def get_stage():
            i = stage_i[0]
            stage_i[0] = (i + 1) % nstage
            return big_f[:, i * 4096:(i + 1) * 4096].rearrange("p (k n) -> p k n", k=8), stage_keys[i], ("stage", i)

        gB = [xres[:, 0, :], xres[:, 1 % NCH, :]]
        gBk = [("xres", 0), ("xres", 1 % NCH)]
        if NCH == 1:
            raise AssertionError("need NCH>=2")
        for nt in range(12):
            sec = nt // 2
            half = nt % 2
            stg, stg_keys, stg_sem = get_stage()
            dma_in("sp", stg, wmod_d[:, nt * 512:(nt + 1) * 512].rearrange("(k p) n -> p k n", p=128), stg_keys, stg_sem)
            bm, bm_k = r2k.get()
            dma_in("sp", bm[:], bmod_d[:, nt * 512:(nt + 1) * 512], [bm_k], ("bm", bm_k[1]))
            b = alloc()
            for k in range(8):
                A("pe", lambda e, k=k, b=b, stg=stg: e.matmul(ps[:, b, :], lhsT=cact[:, k:k + 1].broadcast_to([128, 128]),
                                                              rhs=stg[:, k, :], start=(k == 0), stop=(k == 7)),
                  reads=["cact"] + stg_keys, writes=pk(b))
            if sec in (2, 5):
                gi = 0 if sec == 2 else 1
                A("dve", lambda e, b=b, bm=bm, gi=gi, half=half: e.tensor_tensor(out=gB[gi][:, half * 512:(half + 1) * 512],
                                                                               in0=ps[:, b, :], in1=bm[:], op=ALU.add),
                  reads=pk(b) + [bm_k], writes=[gBk[gi]])
            else:
                col = {0: 0, 1: 1, 3: 2, 4: 3}[sec]
                tmp, tmp_k = r2k.get()
                A("dve", lambda e, b=b, bm=bm, tmp=tmp: e.tensor_tensor(out=tmp[:], in0=ps[:, b, :], in1=bm[:], op=ALU.add),
                  reads=pk(b) + [bm_k], writes=[tmp_k])
                A("dve", lambda e, tmp=tmp: e.tensor_tensor(out=tmp[:].rearrange("p (a i) -> p a i", a=4),
                                                            in0=tmp[:].rearrange("p (a i) -> p a i", a=4),
                                                            in1=ident_f.unsqueeze(1).broadcast_to([128, 4, 128]), op=ALU.mult),
                  reads=[tmp_k, "consts"], writes=[tmp_k])
                A("dve", lambda e, tmp=tmp, col=col, half=half: e.reduce_sum(out=modp[:, col, half * 4:(half + 1) * 4],
                                                                          in_=tmp[:].rearrange("p (a i) -> p a i", a=4), axis=AX.X),
                  reads=[tmp_k], writes=["modp"])
        for col in (1, 3):
            A("dve", lambda e, col=col: e.tensor_scalar(out=modp[:, col, :], in0=modp[:, col, :], scalar1=1.0, scalar2=None, op0=ALU.add),
              reads=["modp"], writes=["modp"])

        def fold_slot(slot, W, r0, c0, gi):
            stg, stg_keys, stg_sem = get_stage()
            dma_in("sp", stg, W[r0:r0 + 1024, c0:c0 + 512].rearrange("(k p) n -> p k n", p=128), stg_keys, stg_sem)
            wt, wt_k = wring.get()
            A("dve", lambda e: e.tensor_tensor(out=wt[:], in0=stg,
                                               in1=gB[gi][:, c0:c0 + 512].unsqueeze(1).broadcast_to([128, 8, 512]), op=ALU.mult),
              reads=stg_keys + [gBk[gi]], writes=[wt_k])
            A("sp", lambda e: e.dma_start(out=wsl_d[slot].rearrange("p (k n) -> p k n", k=8), in_=wt[:]),
              reads=[wt_k], writes=[("wsl", slot)], dma=True, semkey=("wsl", slot))
            wring.release(wt_k)

        for j in range(2):
            fold_slot(SLOTS["out"][0] + j, wout_d, 0, 512 * j, 0)
        for n in range(2):
            for kg in range(4):
                fold_slot(SLOTS["ff2"][0] + 4 * n + kg, wff2_d, 1024 * kg, 512 * n, 1)

        if DEBUG_BARRIER:
            S_.barrier()
        def wload(slot):
            wt, wt_k = wring.get()
            A("sp", lambda e: e.dma_start(out=wt[:].rearrange("p k n -> p (k n)"), in_=wsl_d[slot]),
              reads=[("wsl", slot)], writes=[wt_k], dma=True, semkey=wt_k)
            return wt, wt_k

        class Prefetch:
            def __init__(self):
                sl = [SLOTS["u"][0], SLOTS["u"][0] + 1, SLOTS["v"][0], SLOTS["v"][0] + 1]
                sl += [SLOTS["xbc"][0] + j for j in range(8)] + [SLOTS["z"][0] + j for j in range(4)]
                for j in range(2):
                    sl += [SLOTS["ga"][0] + j, SLOTS["gm"][0] + j, SLOTS["gb"][0] + j, SLOTS["ssm"][0] + 2 * j, SLOTS["ssm"][0] + 2 * j + 1]
                self.slots = sl
                self.loaded = {}
                self.nopen = 0

            def get(self, i):
                for j in (i, i + 1, i + 2):
                    if j < len(self.slots) and j not in self.loaded and (j == i or self.nopen < 3):
                        self.loaded[j] = wload(self.slots[j])
                        self.nopen += 1
                return self.loaded[i]

            def done(self, i):
                wring.release(self.loaded[i][1])
                self.nopen -= 1

        def rstd_from_ss(ss, ss_k, n_feat, ring=None):
            r, r_k = (ring or rst).get()
            A("pool", lambda e: e.tensor_scalar(out=r[:, 0:1], in0=ss, scalar1=1.0 / n_feat, scalar2=EPS, op0=ALU.mult, op1=ALU.add),
              reads=[ss_k], writes=[r_k])
            A("pool", lambda e: e.tensor_tensor(out=r[:, 1:2], in0=r[:, 0:1], in1=mhalf[:], op=ALU.pow),
              reads=[r_k, "mhalf"], writes=[r_k])
            return r[:, 1:2], r_k

        jk2b = jk2[:].bitcast(BF16)

        def norm_multi(jobs):
            norm_tail(jobs, norm_head(jobs))

        def norm_head(jobs):
            st_ = []
            for (c, src, src_k, cs, cb, dst, dst_key, yth) in jobs:
                ring_s = rst_y if yth else rst
                ss, ss_k = ring_s.get()
                A("act", lambda e, src=src, ss=ss: e.activation(out=jk2b, in_=src, func=AF.Square, accum_out=ss[:, 0:1]),
                  reads=[src_k], writes=["jk2", ss_k])
                st_.append([ss, ss_k, ring_s])
            for i, (c, src, src_k, cs, cb, dst, dst_key, yth) in enumerate(jobs):
                ss, ss_k, ring_s = st_[i]
                rs, rs_k = rstd_from_ss(ss[:, 0:1], ss_k, D, ring_s)
                st_[i] += [rs, rs_k]
            for i, (c, src, src_k, cs, cb, dst, dst_key, yth) in enumerate(jobs):
                rs, rs_k = st_[i][3], st_[i][4]
                xb, xb_k = (r2k_y if yth else r2k).get()
                xbv = xb[:].bitcast(BF16)
                A("dve", lambda e, xbv=xbv, src=src, rs=rs: e.tensor_scalar(out=xbv, in0=src, scalar1=rs, scalar2=None, op0=ALU.mult),
                  reads=[src_k, rs_k], writes=[xb_k])
                st_[i] += [xbv, xb_k]
            return st_

        def norm_tail(jobs, st_):
            for i, (c, src, src_k, cs, cb, dst, dst_key, yth) in enumerate(jobs):
                xbv, xb_k = st_[i][5], st_[i][6]
                b = alloc_y() if yth else alloc()
                psb = ps[:, b, :].bitcast(BF16)
                for k in range(8):
                    A("pe", lambda e, k=k, psb=psb, xbv=xbv: e.transpose(out=psb[:, k * 128:(k + 1) * 128], in_=xbv[:, k * 128:(k + 1) * 128], identity=ident_bf),
                      reads=[xb_k, "ident_bf"], writes=pk(b))
                st_[i] += [b, psb]
            for i, (c, src, src_k, cs, cb, dst, dst_key, yth) in enumerate(jobs):
                b, psb = st_[i][7], st_[i][8]
                if yth:
                    for k in range(8):
                        A("dve", lambda e, k=k, psb=psb, dst=dst, c=c, cs=cs, cb=cb: e.tensor_scalar(out=dst[:, k, c * 128:(c + 1) * 128], in0=psb[:, k * 128:(k + 1) * 128],
                                                                                                     scalar1=modp[:, cs, k:k + 1], scalar2=modp[:, cb, k:k + 1],
                                                                                                     op0=ALU.mult, op1=ALU.add),
                          reads=pk(b) + ["modp"], writes=[dst_key])
                    continue
                for k in range(8):
                    A("act", lambda e, k=k, psb=psb, dst=dst, c=c, cs=cs, cb=cb: e.activation(out=dst[:, k, c * 128:(c + 1) * 128], in_=psb[:, k * 128:(k + 1) * 128], func=AF.Identity,
                                                                                              scale=modp[:, cs, k:k + 1], bias=modp[:, cb, k:k + 1]),
                      reads=pk(b) + ["modp"], writes=[dst_key])

        hT_keys = [("hT", c) for c in range(NCH)]

        def fm_matmuls(W, W_k, f, rhs_buf, rhs_keys, b):
            for k in range(8):
                A("pe", lambda e, k=k: e.matmul(ps[:, b, 0:T], lhsT=W[:, k, f * 128:(f + 1) * 128], rhs=rhs_buf[:, k, :],
                                                start=(k == 0), stop=(k == 7)),
                  reads=[W_k] + rhs_keys, writes=pk(b))

        def conv_tile_front(W, W_k, f, t):
            b = alloc()
            fm_matmuls(W, W_k, f, hT, hT_keys, b)
            raw, raw_k = rraw.get()
            acc, acc_k = rg.get()
            A("act", lambda e: e.activation(out=raw[:, 4:4 + T], in_=ps[:, b, 0:T], func=AF.Copy), reads=pk(b), writes=[raw_k])
            A("act", lambda e: e.activation(out=acc[:, 0:T], in_=ps[:, b, 0:T], func=AF.Identity,
                                            scale=pp[:, PP_CW + 4 * t + 3:PP_CW + 4 * t + 4], bias=pp[:, PP_CB + t:PP_CB + t + 1]),
              reads=pk(b) + ["pp"], writes=[acc_k])
            A("pool", lambda e: e.tensor_copy(out=raw[:, 0:4], in_=tails[:, t, :]), reads=[("tails", t)], writes=[raw_k])
            A("pool", lambda e: e.tensor_copy(out=tails[:, t, :], in_=raw[:, T:T + 4]), reads=[raw_k], writes=[("tails", t)])
            return raw, raw_k, acc, acc_k

        def conv_tap(raw, raw_k, acc, acc_k, t, kk):
            sh = 1 + kk
            A("dve", lambda e: e.scalar_tensor_tensor(out=acc[:, 0:T], in0=raw[:, sh:sh + T], scalar=pp[:, PP_CW + 4 * t + kk:PP_CW + 4 * t + kk + 1],
                                                      in1=acc[:, 0:T], op0=ALU.mult, op1=ALU.add),
              reads=[raw_k, acc_k, "pp"], writes=[acc_k])

        def phase_A_jobs(blk):
            t0 = blk * T
            jobs = []
            for c in range(NCH):
                xa, xa_k = r4k.get()
                dma_in("sp", xa[:], x_d[t0 + c * 128:t0 + (c + 1) * 128, :], [xa_k], xa_k)
                jobs.append((c, xa[:], xa_k, 1, 0, hT, ("hT", c), False))
            return jobs

        def phase_AB(blk, pf):
            t0 = blk * T
            for j in range(2):
                W, W_k = pf.get(j)
                for f in range(4):
                    ft = 4 * j + f
                    b = alloc()
                    fm_matmuls(W, W_k, f, hT, hT_keys, b)
                    A("act", lambda e, b=b, ft=ft: e.activation(out=guT[:, ft, :], in_=ps[:, b, 0:T], func=AF.Gelu_apprx_tanh),
                      reads=pk(b), writes=[("guT", ft)])
                    yield 0.5
                pf.done(j)
            Wv = [pf.get(2), pf.get(3)]
            for c in range(NCH):
                b = alloc(2)
                for j in range(2):
                    for k in range(8):
                        A("pe", lambda e, k=k, j=j, c=c, b=b, Wj=Wv[j][0]: e.matmul(ps[:, b + j, :], lhsT=hT[:, k, c * 128:(c + 1) * 128], rhs=Wj[:, k, :],
                                                                                   start=(k == 0), stop=(k == 7)),
                          reads=[Wv[j][1], ("hT", c)], writes=pk(b + j))
                gv, gv_k = r4k.get()
                A("act", lambda e, b=b, gv=gv: e.activation(out=gv[:], in_=ps[:, b:b + 2, :].rearrange("p a n -> p (a n)"), func=AF.Gelu_apprx_tanh),
                  reads=pk(b, 2), writes=[gv_k])
                ss, ss_k = rst.get()
                A("act", lambda e, gv=gv, ss=ss: e.activation(out=jk2b, in_=gv[:], func=AF.Square, accum_out=ss[:, 0:1]),
                  reads=[gv_k], writes=["jk2", ss_k])
                rs, rs_k = rstd_from_ss(ss[:, 0:1], ss_k, D)
                A("dve", lambda e, gv=gv, rs=rs, c=c: e.scalar_tensor_tensor(out=vn[:, c, :], in0=gv[:], scalar=rs, in1=bp[:, BP_GMNW:BP_GMNW + D],
                                                                             op0=ALU.mult, op1=ALU.mult),
                  reads=[gv_k, rs_k, "bp"], writes=[("vn", c)])
                yield 1.0
            pf.done(2)
            pf.done(3)
            for c in range(NCH):
                b = alloc()
                for k in range(8):
                    A("pe", lambda e, k=k, c=c, b=b: e.matmul(ps[:, b, 0:32], lhsT=hT[:, k, c * 128:(c + 1) * 128], rhs=wdt[:, k, :],
                                                              start=(k == 0), stop=(k == 7)),
                      reads=["wdt", ("hT", c)], writes=pk(b))
                A("dve", lambda e, c=c, b=b: e.tensor_tensor(out=dtx[:, c, :], in0=ps[:, b, 0:32], in1=bp[:, BP_DTB:BP_DTB + 32], op=ALU.add),
                  reads=pk(b) + ["bp"], writes=["dtx"])
            dtx_f = dtx[:].rearrange("p c h -> p (c h)")
            dtt_f = dtt[:].rearrange("p c h -> p (c h)")
            dtl_f = dtl[:].rearrange("p c h -> p (c h)")
            A("act", lambda e: e.activation(out=dtt_f, in_=dtx_f, func=AF.Abs), reads=["dtx"], writes=["dtt"])
            A("act", lambda e: e.activation(out=dtl_f, in_=dtt_f, func=AF.Exp, scale=-1.0), reads=["dtt"], writes=["dtl"])
            A("act", lambda e: e.activation(out=dtl_f, in_=dtl_f, func=AF.Ln, bias=1.0), reads=["dtl"], writes=["dtl"])
            A("dve", lambda e: e.scalar_tensor_tensor(out=dtt_f, in0=dtx_f, scalar=0.0, in1=dtl_f, op0=ALU.max, op1=ALU.add),
              reads=["dtx", "dtl"], writes=["dtt"])
            A("dve", lambda e: e.tensor_tensor(out=a_all[:], in0=dtt[:], in1=negA[:].unsqueeze(1).broadcast_to([128, NCH, 32]), op=ALU.mult),
              reads=["dtt", "negA"], writes=["a_all"])
            A("dve", lambda e: e.tensor_copy(out=a_hi[:], in_=a_all[:]), reads=["a_all"], writes=["a_hi"])
            yield 0.5
            for j in range(8):
                W, W_k = pf.get(4 + j)
                for fp in range(2):
                    tl = [4 * j + 2 * fp, 4 * j + 2 * fp + 1]
                    fr = [conv_tile_front(W, W_k, 2 * fp + i, tl[i]) for i in range(2)]
                    for kk in (2, 1, 0):
                        for i in range(2):
                            conv_tap(fr[i][0], fr[i][1], fr[i][2], fr[i][3], tl[i], kk)
                    for i in range(2):
                        A("act", lambda e, acc=fr[i][2], t=tl[i]: e.activation(out=big[:, t, :], in_=acc[:, 0:T], func=AF.Silu),
                          reads=[fr[i][3]], writes=[("big", tl[i])])
                    yield 1.2
                pf.done(4 + j)
            for j in range(4):
                W, W_k = pf.get(12 + j)
                for c in range(NCH):
                    b = alloc()
                    for k in range(8):
                        A("pe", lambda e, k=k, c=c, b=b, W=W: e.matmul(ps[:, b, :], lhsT=hT[:, k, c * 128:(c + 1) * 128], rhs=W[:, k, :],
                                                                      start=(k == 0), stop=(k == 7)),
                          reads=[W_k, ("hT", c)], writes=pk(b))
                    A("act", lambda e, b=b, c=c, j=j: e.activation(out=sz[:, c, j * 512:(j + 1) * 512], in_=ps[:, b, :], func=AF.Silu),
                      reads=pk(b), writes=[("sz", c)])
                    yield 0.3
                pf.done(12 + j)

        def chunk_pre(c):
            csl = slice(c * 128, (c + 1) * 128)
            b = alloc(2)
            for g in range(8):
                bb = b + g // 4
                col = (g % 4) * 128
                A("pe", lambda e, g=g, bb=bb, col=col: e.matmul(ps[:, bb, col:col + 128], lhsT=vn[:, c, g * 128:(g + 1) * 128], rhs=wsTc[:, g, :],
                                                               start=(g % 4 == 0), stop=False, skip_group_check=True),
                  reads=[("vn", c), "wsTc"], writes=pk(bb))
                for hl in range(2):
                    A("pe", lambda e, g=g, bb=bb, col=col, hl=hl: e.matmul(ps[:, bb, col:col + 128], lhsT=cbf[0:8, 0, g:g + 1].broadcast_to([8, 128]), rhs=bsmat[:, hl, :],
                                                                          start=False, stop=(hl == 1), skip_group_check=True),
                      reads=["ident_bf", "bsH"], writes=pk(bb))
            A("dve", lambda e: e.tensor_tensor(out=y_aT[:, :, csl], in0=ps[:, b:b + 2, :].rearrange("p a (g i) -> p (a g) i", g=4),
                                               in1=guT[:, :, csl], op=ALU.mult),
              reads=pk(b, 2) + [("guT", ft) for ft in range(8)], writes=[("y_aT", c)])
            b2 = alloc()
            A("pe", lambda e: e.matmul(ps[:, b2, 0:32], lhsT=Umat, rhs=a_all[:, c, :], start=True, stop=True, skip_group_check=True),
              reads=["consts", "a_all"], writes=pk(b2))
            A("pe", lambda e: e.matmul(ps[:, b2, 32:64], lhsT=ones_f, rhs=a_all[:, c, :], start=False, stop=True, skip_group_check=True),
              reads=["consts", "a_all"], writes=pk(b2))
            cl, ecl, wend = cl_buf, ecl_buf, wend_buf
            A("dve", lambda e: e.tensor_copy(out=cl[:], in_=ps[:, b2, 0:64]), reads=pk(b2), writes=["cl"])
            A("act", lambda e: e.activation(out=ecl[:], in_=cl[:], func=AF.Exp), reads=["cl"], writes=["ecl"])
            A("dve", lambda e: e.tensor_tensor(out=wend[:, 0:32], in0=cl[:, 32:64], in1=cl[:, 0:32], op=ALU.subtract), reads=["cl"], writes=["wend"])
            A("act", lambda e: e.activation(out=wend[:, 32:64], in_=wend[:, 0:32], func=AF.Exp), reads=["wend"], writes=["wend"])
            b3 = alloc(2)
            for t in range(16):
                bb = b3 + t // 8
                psb = ps[:, bb, :].bitcast(BF16)
                A("pe", lambda e, t=t, psb=psb: e.transpose(out=psb[:, (t % 8) * 128:(t % 8 + 1) * 128], in_=big[:, t, csl], identity=ident_bf),
                  reads=[("big", t), "ident_bf"], writes=pk(bb))
            for hb in range(2):
                psb = ps[:, b3 + hb, :].bitcast(BF16)
                A("dve", lambda e, psb=psb, hb=hb: e.tensor_tensor(out=xs_tok[:, hb * 1024:(hb + 1) * 1024].rearrange("p (h d) -> p h d", h=16),
                                                                   in0=psb.rearrange("p (h d) -> p h d", h=16),
                                                                   in1=dtt[:, c, hb * 16:(hb + 1) * 16].unsqueeze(2).broadcast_to([128, 16, 64]), op=ALU.mult),
                  reads=pk(b3 + hb) + ["dtt"], writes=["xs_tok"])
                A("pool", lambda e, hb=hb: e.tensor_tensor(out=xw[:, hb * 1024:(hb + 1) * 1024].rearrange("p (h d) -> p h d", h=16),
                                                           in0=xs_tok[:, hb * 1024:(hb + 1) * 1024].rearrange("p (h d) -> p h d", h=16),
                                                           in1=wend[:, 32 + hb * 16:32 + (hb + 1) * 16].unsqueeze(2).broadcast_to([128, 16, 64]), op=ALU.mult),
                  reads=["xs_tok", "wend"], writes=["xw"])
            b4 = alloc()
            psb4 = ps[:, b4, :].bitcast(BF16)
            for g in range(8):
                A("pe", lambda e, g=g: e.transpose(out=psb4[:, g * 128:(g + 1) * 128], in_=big[:, 16 + g, csl], identity=ident_bf),
                  reads=[("big", 16 + g), "ident_bf"], writes=pk(b4))
            A("act", lambda e: e.activation(out=B_tok[:], in_=psb4, func=AF.Copy), reads=pk(b4), writes=["B_tok"])
            b5 = alloc(2)
            for g in range(8):
                bb = b5 + g // 4
                col = (g % 4) * 128
                A("pe", lambda e, g=g, bb=bb, col=col: e.matmul(ps[:, bb, col:col + 128], lhsT=big[:, 16 + g, csl], rhs=big[:, 24 + g, csl],
                                                               start=(g % 4 == 0), stop=True, skip_group_check=True),
                  reads=[("big", 16 + g), ("big", 24 + g)], writes=pk(bb))
            A("dve", lambda e: e.tensor_tensor(out=cbm[:].rearrange("p (g i) -> p g i", g=8),
                                               in0=ps[:, b5:b5 + 2, :].rearrange("p a (g i) -> p (a g) i", g=4),
                                               in1=Umat.unsqueeze(1).broadcast_to([128, 8, 128]), op=ALU.mult),
              reads=pk(b5, 2) + ["consts"], writes=["cbm"])

        def build_aU(c, q):
            A("dve", lambda e: e.tensor_tensor(out=aUh[:].rearrange("p (h i) -> p h i", h=8),
                                               in0=a_hi[:, c, q * 8:(q + 1) * 8].unsqueeze(2).broadcast_to([128, 8, 128]),
                                               in1=Ubf.unsqueeze(1).broadcast_to([128, 8, 128]), op=ALU.mult),
              reads=["a_hi", "ident_bf"], writes=["aUh"])

        def stA(c, g, ctx):
            gl = g % 2
            b = alloc()
            A("pe", lambda e: e.matmul(ps[:, b, :], lhsT=Vbf, rhs=aUh[:, gl * 512:(gl + 1) * 512], start=True, stop=True),
              reads=["ident_bf", "aUh"], writes=pk(b))
            E, E_k = rE.get()
            A("act", lambda e: e.activation(out=E[:], in_=ps[:, b, :], func=AF.Exp), reads=pk(b), writes=[E_k])
            AT, AT_k = rAT.get()
            ATv = AT[:].rearrange("p (h i) -> p h i", h=4)
            A("dve", lambda e: e.tensor_tensor(out=ATv, in0=E[:].rearrange("p (h i) -> p h i", h=4),
                                               in1=cbm[:, g * 128:(g + 1) * 128].unsqueeze(1).broadcast_to([128, 4, 128]), op=ALU.mult),
              reads=[E_k, "cbm"], writes=[AT_k])
            ctx["AT"] = (ATv, AT_k)

        def stB(c, g, ctx):
            csl = slice(c * 128, (c + 1) * 128)
            ATv, AT_k = ctx["AT"]
            ecl = ecl_buf
            by = alloc()
            for i in range(2):
                t = 2 * g + i
                A("pe", lambda e, i=i, t=t: e.matmul(ps[:, by, i * 128:(i + 1) * 128], lhsT=big[:, t, csl], rhs=diagD[:, t, :],
                                                     start=(i == 0), stop=False, skip_group_check=True),
                  reads=[("big", t), "diagD"], writes=pk(by))
            for hh in range(4):
                h = 4 * g + hh
                A("pe", lambda e, hh=hh, h=h: e.matmul(ps[:, by, hh * 64:(hh + 1) * 64], lhsT=ATv[:, hh, :], rhs=xs_tok[:, h * 64:(h + 1) * 64],
                                                       start=False, stop=(hh == 3), skip_group_check=True),
                  reads=[AT_k, "xs_tok"], writes=pk(by))
            A("pe", lambda e: e.matmul(ps[:, by, 256:512], lhsT=big[:, 24 + g, csl], rhs=Sbf[:, g * 256:(g + 1) * 256],
                                       start=False, stop=True, skip_group_check=True),
              reads=[("big", 24 + g), ("Sbf", g)], writes=pk(by))
            bs_ = alloc()
            A("pe", lambda e: e.matmul(ps[:, bs_, 0:256], lhsT=B_tok[:, g * 128:(g + 1) * 128], rhs=xw[:, g * 256:(g + 1) * 256], start=True, stop=True),
              reads=["B_tok", "xw"], writes=pk(bs_))
            yi, yi_k = ryi.get()
            A("dve", lambda e: e.tensor_tensor(out=yi[:].rearrange("p (h d) -> p h d", h=4), in0=ps[:, by, 256:512].rearrange("p (h d) -> p h d", h=4),
                                               in1=ecl[:, 4 * g:4 * g + 4].unsqueeze(2).broadcast_to([128, 4, 64]), op=ALU.mult),
              reads=pk(by) + ["ecl"], writes=[yi_k])
            A("dve", lambda e: e.tensor_tensor(out=yi[:], in0=ps[:, by, 0:256], in1=yi[:], op=ALU.add), reads=pk(by) + [yi_k], writes=[yi_k])
            A("dve", lambda e: e.tensor_tensor(out=yi[:], in0=yi[:], in1=sz[:, c, g * 256:(g + 1) * 256], op=ALU.mult),
              reads=[yi_k, ("sz", c)], writes=[yi_k])
            Sg = Sst[:, g * 256:(g + 1) * 256]
            A("dve", lambda e: e.tensor_tensor(out=Sg.rearrange("p (h d) -> p h d", h=4), in0=Sg.rearrange("p (h d) -> p h d", h=4),
                                               in1=ecl[:, 32 + 4 * g:32 + 4 * g + 4].unsqueeze(2).broadcast_to([128, 4, 64]), op=ALU.mult),
              reads=[("Sst", g), "ecl"], writes=[("Sst", g)])
            A("dve", lambda e: e.tensor_tensor(out=Sg, in0=ps[:, bs_, 0:256], in1=Sg, op=ALU.add), reads=pk(bs_) + [("Sst", g)], writes=[("Sst", g)])
            A("act", lambda e: e.activation(out=Sbf[:, g * 256:(g + 1) * 256], in_=Sg, func=AF.Copy), reads=[("Sst", g)], writes=[("Sbf", g)])
            ctx["yi"] = (yi, yi_k)

        def stC(c, g, ctx):
            yi, yi_k = ctx["yi"]
            sg, sg_k = rst.get()
            A("act", lambda e: e.activation(out=jk2[:, 0:256], in_=yi[:], func=AF.Square, accum_out=sg[:, 0:1]), reads=[yi_k], writes=["jk2", sg_k])
            ctx["rs"] = rstd_from_ss(sg[:, 0:1], sg_k, 256)

        def stC2(c, g, ctx):
            yi, yi_k = ctx["yi"]
            rs, rs_k = ctx["rs"]
            gs, gs_k = rgs.get()
            A("act", lambda e: e.activation(out=gs[:], in_=yi[:], func=AF.Identity, scale=rs), reads=[yi_k, rs_k], writes=[gs_k])
            ctx["gs"] = (gs, gs_k)

        def stD(c, g, ctx):
            csl = slice(c * 128, (c + 1) * 128)
            gs, gs_k = ctx["gs"]
            bb = alloc()
            psb = ps[:, bb, :].bitcast(BF16)
            for i in range(2):
                A("pe", lambda e, i=i: e.transpose(out=psb[:, i * 128:(i + 1) * 128], in_=gs[:, i * 128:(i + 1) * 128], identity=ident_bf),
                  reads=[gs_k, "ident_bf"], writes=pk(bb))
            for i in range(2):
                t = 2 * g + i
                A("act", lambda e, t=t, i=i: e.activation(out=y_bT[:, t, csl], in_=psb[:, i * 128:(i + 1) * 128], func=AF.Identity,
                                                          scale=pp[:, PP_NW + t:PP_NW + t + 1]),
                  reads=pk(bb) + ["pp"], writes=[("y_bT", c)])

        def phase_C(blk):
            for c in range(NCH):
                chunk_pre(c)
                yield 2.0
                ctxs = [dict() for _ in range(8)]
                build_aU(c, 0)
                for step in range(8 + 3):
                    if 0 <= step - 2 < 8:
                        stC(c, step - 2, ctxs[step - 2])
                    if 0 <= step - 1 < 8:
                        stB(c, step - 1, ctxs[step - 1])
                    if step < 8:
                        stA(c, step, ctxs[step])
                        if step in (1, 3, 5):
                            build_aU(c, (step + 1) // 2)
                    yield 1.2
                    for _ in range(NFILL):
                        A("pe", lambda e: e.matmul(ps[:, 4, :], lhsT=ident_bf, rhs=wsTc[:, 0:4, :].rearrange("p g i -> p (g i)"), start=True, stop=True),
                          reads=["wsTc", "ident_bf"])
                    if 0 <= step - 2 < 8:
                        stC2(c, step - 2, ctxs[step - 2])
                    if 0 <= step - 3 < 8:
                        stD(c, step - 3, ctxs[step - 3])

        y_aT_keys = [("y_aT", c) for c in range(NCH)]
        y_bT_keys = [("y_bT", c) for c in range(NCH)]

        def phase_D(blk, pf, hook=None):
            if hook is not None:
                hook(0)
            for j in range(2):
                W, W_k = pf.get(16 + 5 * j)
                sa = []
                for f in range(4):
                    b = alloc()
                    fm_matmuls(W, W_k, f, hT, hT_keys, b)
                    s_, s_k = rg.get()
                    A("act", lambda e, b=b, s_=s_: e.activation(out=s_[:, 0:T], in_=ps[:, b, 0:T], func=AF.Sigmoid), reads=pk(b), writes=[s_k])
                    sa.append((s_, s_k))
                    yield 0.1
                pf.done(16 + 5 * j)
                W, W_k = pf.get(17 + 5 * j)
                for f in range(4):
                    b = alloc()
                    fm_matmuls(W, W_k, f, y_aT, y_aT_keys, b)
                    s_, s_k = sa[f]
                    A("dve", lambda e, b=b, s_=s_: e.tensor_tensor(out=s_[:, 0:T], in0=ps[:, b, 0:T], in1=s_[:, 0:T], op=ALU.mult),
                      reads=pk(b) + [s_k], writes=[s_k])
                    yield 0.1
                pf.done(17 + 5 * j)
                W, W_k = pf.get(18 + 5 * j)
                sbb = []
                for f in range(4):
                    b = alloc()
                    fm_matmuls(W, W_k, f, hT, hT_keys, b)
                    s_, s_k = rg.get()
                    A("act", lambda e, b=b, s_=s_: e.activation(out=s_[:, 0:T], in_=ps[:, b, 0:T], func=AF.Sigmoid), reads=pk(b), writes=[s_k])
                    sbb.append((s_, s_k))
                    yield 0.1
                pf.done(18 + 5 * j)
                if j == 1 and hook is not None:
                    hook(1)
                W0, W0_k = pf.get(19 + 5 * j)
                W1, W1_k = pf.get(20 + 5 * j)
                for f in range(4):
                    b = alloc()
                    for kk in range(16):
                        Wx, Wx_k = (W0, W0_k) if kk < 8 else (W1, W1_k)
                        A("pe", lambda e, kk=kk, Wx=Wx, f=f, b=b: e.matmul(ps[:, b, 0:T], lhsT=Wx[:, kk % 8, f * 128:(f + 1) * 128], rhs=y_bT[:, kk, :],
                                                                          start=(kk == 0), stop=(kk == 15)),
                          reads=[Wx_k] + y_bT_keys, writes=pk(b))
                    s_, s_k = sbb[f]
                    A("dve", lambda e, b=b, s_=s_: e.tensor_tensor(out=s_[:, 0:T], in0=ps[:, b, 0:T], in1=s_[:, 0:T], op=ALU.mult),
                      reads=pk(b) + [s_k], writes=[s_k])
                    a_, a_k = sa[f]
                    ft = 4 * j + f
                    A("dve", lambda e, s_=s_, a_=a_, ft=ft: e.tensor_tensor(out=guT[:, ft, :], in0=a_[:, 0:T], in1=s_[:, 0:T], op=ALU.add),
                      reads=[s_k, a_k], writes=[("guT", ft)])
                    yield 0.1
                pf.done(19 + 5 * j)
                pf.done(20 + 5 * j)
            if hook is not None:
                hook(2)

        mixT = guT
        mix_keys = [("guT", ft) for ft in range(8)]

        h2T_keys = [("h2T", c) for c in range(NCH)]

        def x_reread(blk):
            t0 = blk * T
            for c in range(NCH):
                dma_in("pool", xres[:, c, :], x_d[t0 + c * 128:t0 + (c + 1) * 128, :], [("xres", c)], ("xres", c))

        def phase_E_mm(blk, Wpre=None, x_done=False):
            if not x_done:
                x_reread(blk)
            for n in range(2):
                W, W_k = Wpre[n] if Wpre is not None else wload(SLOTS["out"][0] + n)
                for c in range(NCH):
                    b = alloc() if (n == 1 and c == NCH - 1) else alloc_y()
                    for k in range(8):
                        A("pe", lambda e, k=k, c=c, b=b, W=W: e.matmul(ps[:, b, :], lhsT=mixT[:, k, c * 128:(c + 1) * 128], rhs=W[:, k, :],
                                                                      start=(k == 0), stop=(k == 7)),
                          reads=[W_k] + mix_keys, writes=pk(b))
                    A("dve", lambda e, b=b, c=c, n=n: e.tensor_tensor(out=xres[:, c, n * 512:(n + 1) * 512], in0=ps[:, b, :], in1=xres[:, c, n * 512:(n + 1) * 512], op=ALU.add),
                      reads=pk(b) + [("xres", c)], writes=[("xres", c)])
                wring.release(W_k)
            return [(c, xres[:, c, :], ("xres", c), 3, 2, h2T, ("h2T", c), True) for c in range(NCH)]

        def phase_F(blk):
            for j in range(8):
                W, W_k = wload(SLOTS["ff1"][0] + j)
                for f in range(4):
                    ft = 4 * j + f
                    b = alloc_y()
                    fm_matmuls(W, W_k, f, h2T, h2T_keys, b)
                    r_, r_k = rT_y.get()
                    A("act", lambda e, b=b, r_=r_: e.activation(out=r_[:, 0:T], in_=ps[:, b, 0:T], func=AF.Relu), reads=pk(b), writes=[r_k])
                    A("act", lambda e, r_=r_, ft=ft: e.activation(out=fT[:, ft, :], in_=r_[:, 0:T], func=AF.Square),
                      reads=[r_k], writes=[("fT", ft)])
                    yield 1.0
                wring.release(W_k)

        def phase_G(blk):
            t0 = blk * T
            for n in range(2):
                b0 = alloc_y(NCH)
                for kg in range(4):
                    W, W_k = wload(SLOTS["ff2"][0] + 4 * n + kg)
                    for c in range(NCH):
                        for k in range(8):
                            kt = kg * 8 + k
                            A("pe", lambda e, k=k, kt=kt, c=c, W=W, b0=b0, kg=kg: e.matmul(ps[:, b0 + c, :], lhsT=fT[:, kt, c * 128:(c + 1) * 128], rhs=W[:, k, :],
                                                                                         start=(kg == 0 and k == 0), stop=(kg == 3 and k == 7)),
                              reads=[W_k, ("fT", kt)], writes=pk(b0 + c))
                        yield 1.9
                    wring.release(W_k)
                for c in range(NCH):
                    A("dve", lambda e, c=c, n=n, b0=b0: e.tensor_tensor(out=xres[:, c, n * 512:(n + 1) * 512], in0=ps[:, b0 + c, :], in1=xres[:, c, n * 512:(n + 1) * 512], op=ALU.add),
                      reads=pk(b0 + c) + [("xres", c)], writes=[("xres", c)])
            for c in range(NCH):
                ss, ss_k = rst_y.get()
                A("act", lambda e, ss=ss, c=c: e.activation(out=jk2b, in_=xres[:, c, :], func=AF.Square, accum_out=ss[:, 0:1]),
                  reads=[("xres", c)], writes=["jk2", ss_k])
                rs, rs_k = rstd_from_ss(ss[:, 0:1], ss_k, D, rst_y)
                A("dve", lambda e, rs=rs, c=c: e.scalar_tensor_tensor(out=xres[:, c, :], in0=xres[:, c, :], scalar=rs, in1=bp[:, BP_FNW:BP_FNW + D], op0=ALU.mult, op1=ALU.mult),
                  reads=[("xres", c), rs_k, "bp"], writes=[("xres", c)])
                A("pool", lambda e, c=c, t0=t0: e.dma_start(out=out_d[t0 + c * 128:t0 + (c + 1) * 128, :], in_=xres[:, c, :]),
                  reads=[("xres", c)], dma=True, semkey=("o", c))
                yield 1.0

        def run_all(gen):
            for _ in gen:
                pass

        def chainX(blk, pf, hook=None):
            yield from phase_AB(blk, pf)
            yield from phase_C(blk)
            yield from phase_D(blk, pf, hook)

        def chainY(blk, jobsE=()):
            for jb in jobsE:
                norm_multi([jb])
                yield 1.0
            yield from phase_F(blk)
            yield from phase_G(blk)

        W_TOTAL = 8 * 0.5 + 2 * 1.0 + 0.5 + 16 * 1.2 + 8 * 0.3 + NCH * (2.0 + 11 * 1.2) + 32 * 0.1
        Y_TOTAL = 32 * 1.0 + 16 * 1.9 + 2 * 1.0

        ystate = {"done": True}

        Y_DELAY = 2.0

        def interleave(X, Y):
            acc = 0.0
            started = False
            ynext = None
            ystate["done"] = Y is None
            if X is not None:
                for w in X:
                    acc += w * (Y_TOTAL / W_TOTAL) * 1.06
                    if Y is None:
                        continue
                    if not started:
                        if acc < Y_DELAY:
                            continue
                        started = True
                        acc = 0.0
                        ynext = next(Y, None)
                    while ynext is not None and acc >= ynext:
                        acc -= ynext
                        ynext = next(Y, None)
                    ystate["done"] = ynext is None
            if Y is not None:
                if not started:
                    ynext = next(Y, None)
                while ynext is not None:
                    ynext = next(Y, None)
            ystate["done"] = True

        if OVERLAP:
            early = {}

            def make_hook(blk):
                def hook(stage):
                    nxt = blk + 1 < NBLK
                    if stage == 0 and nxt:
                        early["jobsA"] = phase_A_jobs(blk + 1)
                    elif stage == 1:
                        early["Wout"] = [wload(SLOTS["out"][0] + n) for n in range(2)]
                        if nxt:
                            early["stA"] = norm_head(early["jobsA"])
                    elif stage == 2:
                        if nxt:
                            norm_tail(early.pop("jobsA"), early.pop("stA"))
                        if ystate["done"]:
                            x_reread(blk)
                            early["xre"] = True
                return hook

            norm_multi(phase_A_jobs(0))
            for blk in range(NBLK + 1):
                pf = Prefetch()
                jobsE = []
                if blk < NBLK:
                    pf.get(0)
                if blk >= 1:
                    jobsE = phase_E_mm(blk - 1, early.pop("Wout"), early.pop("xre", False))
                interleave(chainX(blk, pf, make_hook(blk)) if blk < NBLK else None, chainY(blk - 1, jobsE) if blk >= 1 else None)
        else:
            for blk in range(NBLK):
                norm_multi(phase_A_jobs(blk))
                run_all(chainX(blk, Prefetch()))
                for jb in phase_E_mm(blk):
                    norm_multi([jb])
                run_all(chainY(blk))

        A("sp", None, writes=[("xres", c) for c in range(NCH)])
        if DEBUG_DUMP:
            lastd = [o for o in S_.ops if o.dma and isinstance(o.semkey, tuple) and o.semkey[0] == "dbg"]
            op = S_.add("sp", None)
            op.deps = [(p, True) for p in lastd]
        S_.emit(nc)
    print("ops:", {e: len(S_.streams[e]) for e in ENG_NAMES}, "sems:", S_.n_sems)
    return nc


def _host_layout(inputs, b):
    f = np.float32
    c = np.asarray(inputs["c"], f)[b]
    conv_w = np.asarray(inputs["conv_w"], f)[0]
    conv_b = np.asarray(inputs["conv_b"], f)[0]
    ssm_nw = np.asarray(inputs["ssm_norm_w"], f)[0]
    d_skip = np.asarray(inputs["d_skip"], f)[0]
    pp = np.zeros((128, PP_N), f)
    pp[:, PP_C:PP_C + 8] = c.reshape(8, 128).T
    cw = conv_w.reshape(4, 32, 128)
    pp[:, PP_CW:PP_CW + 128] = cw.transpose(2, 1, 0).reshape(128, 128)
    pp[:, PP_CB:PP_CB + 32] = conv_b.reshape(32, 128).T
    pp[:, PP_NW:PP_NW + 16] = ssm_nw.reshape(16, 128).T
    ch = np.arange(2048).reshape(16, 128).T
    pp[:, PP_DS:PP_DS + 16] = d_skip[ch // 64]
    bp = np.zeros((128, BP_N), f)
    bp[:, BP_DTB:BP_DTB + 32] = np.asarray(inputs["dt_bias"], f)[0][None, :]
    bp[:, BP_ALOG:BP_ALOG + 32] = np.asarray(inputs["a_log"], f)[0][None, :]
    bp[:, BP_GMNW:BP_GMNW + D] = np.asarray(inputs["gm_norm_w"], f)[0][None, :]
    bp[:, BP_FNW:BP_FNW + D] = np.asarray(inputs["final_norm_w"], f)[None, :]
    return pp, bp


def _consts():
    k = np.arange(128)
    ident = np.eye(128, dtype=np.float32)
    U = (k[:, None] <= k[None, :]).astype(np.float32)
    V = (k[:, None] > k[None, :]).astype(np.float32)
    ones = np.ones((128, 128), np.float32)
    return np.ascontiguousarray(np.stack([ident, U, V, ones], axis=1))


def run(inputs, S=None, T=256, cores=None, trace=False):
    f = np.float32
    x = np.asarray(inputs["x"], f)
    B = x.shape[0]
    if S is None:
        S = x.shape[1]
    cores = list(range(B)) if cores is None else cores
    nc = build_nc(S, T)
    consts = _consts()
    shared = {
        "w_mod": np.ascontiguousarray(np.asarray(inputs["w_mod"], f)[0]),
        "bmod": np.ascontiguousarray(np.broadcast_to(np.asarray(inputs["b_mod"], f)[0][None, :], (128, 6 * D))),
        "w_in": np.ascontiguousarray(np.asarray(inputs["w_in"], f)[0]),
        "w_gm": np.ascontiguousarray(np.asarray(inputs["w_branch_gm"], f)[0]),
        "w_ssm": np.ascontiguousarray(np.asarray(inputs["w_branch_ssm"], f)[0]),
        "w_out": np.ascontiguousarray(np.asarray(inputs["w_out"], f)[0]),
        "w_ff1": np.ascontiguousarray(np.asarray(inputs["w_ff1"], f)[0]),
        "w_ff2": np.ascontiguousarray(np.asarray(inputs["w_ff2"], f)[0]),
        "consts": consts,
        "bsrow": np.ascontiguousarray(np.asarray(inputs["gm_bs"], f)[0].reshape(8, 128)),
        "wsT": np.ascontiguousarray(np.asarray(inputs["gm_ws"], f)[0].transpose(2, 0, 1)),
    }
    in_maps = []
    for b in cores:
        pp, bp = _host_layout(inputs, b)
        m = dict(shared)
        m["x"] = np.ascontiguousarray(x[b, :S])
        m["pp"] = pp
        m["bp"] = bp
        in_maps.append(m)
    res = run_bass_kernel_spmd(nc, in_maps, core_ids=list(range(len(cores))), trace=trace)
    out = np.stack([np.asarray(r["out"], dtype=f) for r in res.results], axis=0)
    return out, res


def kernel(**inputs):
    out, _ = run(inputs)
    return out
```

```python
import contextlib
import numpy as np
import concourse.bass as bass
import concourse.mybir as mybir
from concourse.bass_utils import run_bass_kernel_spmd

F32 = mybir.dt.float32
BF16 = mybir.dt.bfloat16
AF = mybir.ActivationFunctionType
ALU = mybir.AluOpType
AX = mybir.AxisListType

D = 1024
NKT = 8
DIN = 2048
NH = 32
NG = 8
DFF = 4096
EPS = 1e-6
OFF_U, OFF_V, OFF_Z, OFF_XBC, OFF_DT, OFF_GA, OFF_GB = 0, 1024, 2048, 4096, 8192, 8224, 9248
N_CORES = 8
DEBUG_BARRIER = False
DEBUG_DUMP = False
OVERLAP = True
NFILL = 0

ENG_NAMES = ("pe", "act", "dve", "pool", "sp")
EPOCH = 12000


class Op:
    __slots__ = ("eng", "fn", "reads", "writes", "dma", "semkey", "idx", "deps", "sig", "cnt", "dcount", "waits")

    def __init__(self, eng, fn, reads, writes, dma, semkey):
        self.eng = eng
        self.fn = fn
        self.reads = reads
        self.writes = writes
        self.dma = dma
        self.semkey = semkey
        self.deps = []
        self.sig = False
        self.cnt = 0
        self.dcount = 0
        self.waits = []


class Sched:
    def __init__(self):
        self.ops = []
        self.streams = {e: [] for e in ENG_NAMES}
        self.last_write = {}
        self.readers = {}
        self.dma_counts = {}

    def add(self, eng, fn, reads=(), writes=(), dma=False, semkey=None):
        op = Op(eng, fn, tuple(reads), tuple(writes), dma, semkey)
        if dma:
            assert semkey is not None
            n = self.dma_counts.get(semkey, 0) + 1
            self.dma_counts[semkey] = n
            op.dcount = n
        op.idx = len(self.streams[eng])
        self.streams[eng].append(op)
        self.ops.append(op)
        deps = {}
        for r in op.reads:
            w = self.last_write.get(r)
            if w is not None:
                deps[id(w)] = (w, True)
        for r in op.writes:
            w = self.last_write.get(r)
            if w is not None and id(w) not in deps:
                deps[id(w)] = (w, False)
            for rd in self.readers.get(r, ()):
                if id(rd) not in deps:
                    deps[id(rd)] = (rd, False)
        op.deps = list(deps.values())
        for r in op.reads:
            lst = self.readers.setdefault(r, [])
            if not dma:
                lst[:] = [o for o in lst if o.dma or o.eng != eng]
            lst.append(op)
        for r in op.writes:
            self.last_write[r] = op
            self.readers[r] = []
        return op

    def barrier(self):
        lasts = []
        for e in ENG_NAMES:
            comp = [o for o in self.streams[e] if not o.dma and o.fn is not None]
            if comp:
                lasts.append((comp[-1], True))
        lastd = {}
        for o in self.ops:
            if o.dma:
                lastd[o.semkey] = o
        lasts += [(o, True) for o in lastd.values()]
        for e in ENG_NAMES:
            op = Op(e, None, (), (), False, None)
            op.idx = len(self.streams[e])
            self.streams[e].append(op)
            self.ops.append(op)
            op.deps = [(p, True) for (p, _) in lasts]

    def finalize(self):
        seen = {e: {} for e in ENG_NAMES}
        for op in self.ops:
            need = {}
            for (p, raw) in op.deps:
                if p.dma:
                    k = ("d", p.semkey)
                    v = p.dcount
                else:
                    if p.eng == op.eng and not op.dma and op.fn is not None:
                        if p.eng == "pe":
                            continue
                        if not raw:
                            continue
                    k = ("e", p.eng)
                    v = p.idx + 1
                if seen[op.eng].get(k, 0) >= v:
                    continue
                if need.get(k, (0, None))[0] < v:
                    need[k] = (v, p)
            for k, (v, p) in need.items():
                seen[op.eng][k] = v
                p.sig = True
            op.waits = list(need.items())
        self.nsig = {}
        for e in ENG_NAMES:
            c = 0
            for op in self.streams[e]:
                if op.dma:
                    continue
                if op.sig:
                    c += 1
                    op.cnt = c
            self.nsig[e] = c

    def emit(self, nc):
        self.finalize()
        with contextlib.ExitStack() as st:
            esems = {}
            for e in ENG_NAMES:
                n = self.nsig[e] // EPOCH + 1
                esems[e] = [st.enter_context(nc.semaphore(f"s_{e}_{i}")) for i in range(n)]
            dsems = {}
            for k in self.dma_counts:
                dsems[k] = st.enter_context(nc.semaphore("d_" + str(len(dsems))))
            self.n_sems = sum(len(v) for v in esems.values()) + len(dsems)
            block = st.enter_context(nc.Block())

            def run_stream(ename):
                def body(eng):
                    for op in self.streams[ename]:
                        for k, (v, p) in op.waits:
                            if k[0] == "d":
                                eng.wait_ge(dsems[k[1]], 16 * p.dcount)
                            else:
                                c = p.cnt
                                ep = (c - 1) // EPOCH
                                eng.wait_ge(esems[k[1]][ep], c - ep * EPOCH)
                        if op.fn is None:
                            continue
                        ins = op.fn(eng)
                        if op.dma:
                            ins.then_inc(dsems[op.semkey], 16)
                        elif op.sig:
                            ep = (op.cnt - 1) // EPOCH
                            ins.then_inc(esems[ename][ep], 1)
                return body

            block.tensor(run_stream("pe"))
            block.scalar(run_stream("act"))
            block.vector(run_stream("dve"))
            block.gpsimd(run_stream("pool"))
            block.sync(run_stream("sp"))


class Ring:
    def __init__(self, name, tensors):
        self.name = name
        self.tensors = tensors
        self.i = 0

    def get(self):
        i = self.i
        self.i = (i + 1) % len(self.tensors)
        return self.tensors[i], (self.name, i)


class WRing:
    def __init__(self, name, tensors):
        self.name = name
        self.tensors = tensors
        self.free = list(range(len(tensors)))

    def get(self):
        assert self.free, "weight ring exhausted"
        i = self.free.pop(0)
        return self.tensors[i], (self.name, i)

    def release(self, key):
        assert key[1] not in self.free
        self.free.append(key[1])


SLOTS = {}
_n = 0
for _nm, _c in [("u", 2), ("xbc", 8), ("v", 2), ("z", 4), ("ga", 2), ("gb", 2), ("gm", 2), ("ssm", 4),
                ("out", 2), ("ff1", 8), ("ff2", 8)]:
    SLOTS[_nm] = (_n, _c)
    _n += _c
NSLOT = _n

PP_C, PP_CW, PP_CB, PP_NW, PP_DS, PP_N = 0, 8, 136, 168, 184, 200
BP_DTB, BP_ALOG, BP_GMNW, BP_FNW, BP_N = 0, 32, 64, 1088, 2112


def build_nc(S, T):
    NCH = T // 128
    assert T == 256
    NBLK = S // T
    assert S % T == 0 and T % 128 == 0
    nc = bass.Bass("TRN2", target_bir_lowering=False)

    def din(name, shape, dt=F32):
        return nc.dram_tensor(name, shape, dt, kind="ExternalInput").ap()

    x_d = din("x", [S, D])
    wmod_d = din("w_mod", [D, 6 * D])
    bmod_d = din("bmod", [128, 6 * D])
    win_d = din("w_in", [D, 10272])
    wgm_d = din("w_gm", [D, D])
    wssm_d = din("w_ssm", [DIN, D])
    wout_d = din("w_out", [D, D])
    wff1_d = din("w_ff1", [D, DFF])
    wff2_d = din("w_ff2", [DFF, D])
    consts_d = din("consts", [128, 4, 128])
    pp_d = din("pp", [128, PP_N])
    bp_d = din("bp", [128, BP_N])
    bsrow_d = din("bsrow", [8, 128])
    wsT_d = din("wsT", [128, 8, 128])
    out_d = nc.dram_tensor("out", [S, D], F32, kind="ExternalOutput").ap()
    wsl_d = nc.dram_tensor("wsl", [NSLOT, 128, 4096], BF16, kind="Internal").ap()
    wdt_d = nc.dram_tensor("wdt_s", [128, 256], BF16, kind="Internal").ap()

    S_ = Sched()
    A = S_.add
    dbg = {}

    def dump(name, ap, keys, shape, dt=F32):
        if not DEBUG_DUMP or name in dbg:
            return
        dbg[name] = nc.dram_tensor("dbg_" + name, list(shape), dt, kind="ExternalOutput").ap()
        A("sp", lambda e: e.dma_start(out=dbg[name], in_=ap), reads=keys, dma=True, semkey=("dbg", name))

    with contextlib.ExitStack() as st:
        def sb(name, shape, dt=F32):
            return st.enter_context(nc.sbuf_tensor("s_" + name, shape, dt))

        consts = sb("consts", [128, 4, 128])
        ident_f = consts[:, 0, :]
        Umat = consts[:, 1, :]
        Vmat = consts[:, 2, :]
        ones_f = consts[:, 3, :]
        cbf = sb("cbf", [128, 4, 128], BF16)
        ident_bf = cbf[:, 0, :]
        Ubf = cbf[:, 1, :]
        Vbf = cbf[:, 2, :]
        ones_bf = cbf[:, 3, :]
        pp = sb("pp", [128, PP_N])
        bp = sb("bp", [128, BP_N])
        bsmat = sb("bsmat", [8, 2, 128], BF16)
        wsTc = sb("wsTc", [128, 8, 128], BF16)
        diagD = sb("diagD", [128, 16, 128], BF16)
        wdt = sb("wdt", [128, 8, 32], BF16)
        negA = sb("negA", [128, 32])
        cact = sb("cact", [128, 8])
        modp = sb("modp", [128, 4, 8])
        mhalf = sb("mhalf", [128, 1])
        epsc = sb("epsc", [128, 1])
        Sst = sb("Sst", [128, 2048])
        Sbf = sb("Sbf", [128, 2048], BF16)
        tails = sb("tails", [128, 32, 4])
        xres = sb("xres", [128, NCH, D])
        hT = sb("hT", [128, NKT, T], BF16)
        h2T = sb("h2T", [128, NKT, T], BF16)
        fT = sb("fT", [128, 32, T], BF16)
        guT = sb("guT", [128, NKT, T], BF16)
        big = sb("big", [128, 32, T], BF16)
        vn = sb("vn", [128, NCH, D], BF16)
        sz = sb("sz", [128, NCH, DIN], BF16)
        dtx = sb("dtx", [128, NCH, 32])
        dtt = sb("dtt", [128, NCH, 32])
        dtl = sb("dtl", [128, NCH, 32])
        a_all = sb("a_all", [128, NCH, 32])
        y_aT = sb("y_aT", [128, NKT, T], BF16)
        y_bT = sb("y_bT", [128, 16, T], BF16)
        xs_tok = sb("xs_tok", [128, DIN], BF16)
        B_tok = sb("B_tok", [128, 1024], BF16)
        xw = sb("xw", [128, DIN], BF16)
        cl_buf = sb("cl_buf", [128, 64])
        ecl_buf = sb("ecl_buf", [128, 64])
        wend_buf = sb("wend_buf", [128, 64])
        a_hi = sb("a_hi", [128, NCH, 32], BF16)
        aUh = sb("aUh", [128, 1024], BF16)
        cbm = sb("cbm", [128, 1024], BF16)
        jk2 = sb("jk2", [128, 512])
        r2k = Ring("r2k", [sb(f"r2k{i}", [128, 512]) for i in range(2)])
        r2k_y = Ring("r2ky", [sb(f"r2ky{i}", [128, 512]) for i in range(1)])
        rg = Ring("rg", [sb(f"rg{i}", [128, 256]) for i in range(8)])
        rT_y = Ring("rTy", [sb(f"rTy{i}", [128, 256]) for i in range(2)])
        rE = Ring("rE", [sb(f"rE{i}", [128, 512]) for i in range(2)])
        rAT = Ring("rAT", [sb(f"rAT{i}", [128, 512], BF16) for i in range(3)])
        ryi = Ring("ryi", [sb(f"ryi{i}", [128, 256]) for i in range(3)])
        rgs = Ring("rgs", [sb(f"rgs{i}", [128, 256], BF16) for i in range(3)])
        r4k = Ring("r4k", [sb(f"r4k{i}", [128, 1024]) for i in range(2)])
        rraw = Ring("raw", [sb(f"raw{i}", [128, T + 4]) for i in range(3)])
        rst = Ring("st", [sb(f"st{i}", [128, 64]) for i in range(8)])
        rst_y = Ring("sty", [sb(f"sty{i}", [128, 64]) for i in range(4)])
        wring = WRing("wr", [sb(f"wr{i}", [128, 8, 512], BF16) for i in range(5)])
        ps = st.enter_context(nc.psum_tensor("ps", [128, 8, 512], F32))
        print("sbuf bytes remaining:", nc.sbuf_bytes_remaining)

        bank_ptr = [0]
        NXB = 5

        def alloc(n=1):
            b = bank_ptr[0]
            if b + n > NXB:
                b = 0
            bank_ptr[0] = (b + n) % NXB
            return b

        bank_ptr_y = [0]

        def alloc_y(n=1):
            b = bank_ptr_y[0]
            if b + n > 3:
                b = 0
            bank_ptr_y[0] = (b + n) % 3
            return 5 + b

        def pk(b, n=1):
            return [("ps", b + i) for i in range(n)]

        def dma_in(eng, out_ap, in_ap, wkeys, semkey, rkeys=()):
            A(eng, lambda e: e.dma_start(out=out_ap, in_=in_ap), reads=rkeys, writes=wkeys, dma=True, semkey=semkey)

        dma_in("sp", consts[:], consts_d, ["consts"], "c_consts")
        dma_in("sp", pp[:], pp_d, ["pp"], "c_pp")
        dma_in("sp", bp[:], bp_d, ["bp"], "c_bp")
        bs_st, bs_k = r4k.get()
        dma_in("sp", bs_st[0:8, 0:128], bsrow_d, [bs_k], "c_bsrow")
        wsT_st, wsT_k = r4k.get()
        dma_in("sp", wsT_st[:].rearrange("p (g i) -> p g i", g=8), wsT_d, [wsT_k], "c_wsT")

        def cast_slot(slot, W, r0, c0):
            src = W[r0:r0 + 1024, c0:c0 + 512].rearrange("(k p) n -> p k n", p=128)
            dst = wsl_d[slot].rearrange("p (k n) -> p k n", k=8)
            A("pool", lambda e: e.dma_start(out=dst, in_=src), writes=[("wsl", slot)], dma=True, semkey=("wsl", slot))

        for j in range(2):
            cast_slot(SLOTS["u"][0] + j, win_d, 0, OFF_U + 512 * j)
        for j in range(8):
            cast_slot(SLOTS["xbc"][0] + j, win_d, 0, OFF_XBC + 512 * j)
        for j in range(2):
            cast_slot(SLOTS["v"][0] + j, win_d, 0, OFF_V + 512 * j)
        for j in range(4):
            cast_slot(SLOTS["z"][0] + j, win_d, 0, OFF_Z + 512 * j)
        A("pool", lambda e: e.dma_start(out=wdt_d.rearrange("p (k n) -> p k n", k=8),
                                        in_=win_d[:, OFF_DT:OFF_DT + 32].rearrange("(k p) n -> p k n", p=128)),
          writes=["wdt_d"], dma=True, semkey="wdt_d")
        dma_in("sp", wdt[:], wdt_d.rearrange("p (k n) -> p k n", k=8), ["wdt"], "c_wdt", rkeys=["wdt_d"])
        for j in range(2):
            cast_slot(SLOTS["ga"][0] + j, win_d, 0, OFF_GA + 512 * j)
        for j in range(2):
            cast_slot(SLOTS["gm"][0] + j, wgm_d, 0, 512 * j)
        for j in range(2):
            cast_slot(SLOTS["gb"][0] + j, win_d, 0, OFF_GB + 512 * j)
        for j in range(2):
            for kh in range(2):
                cast_slot(SLOTS["ssm"][0] + 2 * j + kh, wssm_d, 1024 * kh, 512 * j)
        for j in range(8):
            cast_slot(SLOTS["ff1"][0] + j, wff1_d, 0, 512 * j)

        A("dve", lambda e: e.tensor_copy(out=cbf[:], in_=consts[:]), reads=["consts"], writes=["ident_bf"])
        A("dve", lambda e: e.tensor_copy(out=bsmat[:, 0, :], in_=bs_st[0:8, 0:128]), reads=[bs_k], writes=["bsH"])
        A("dve", lambda e: e.tensor_tensor(out=bsmat[:, 1, :], in0=bs_st[0:8, 0:128], in1=bsmat[:, 0, :], op=ALU.subtract),
          reads=[bs_k, "bsH"], writes=["bsH"])
        A("dve", lambda e: e.tensor_tensor(out=wsTc[:], in0=wsT_st[:].rearrange("p (g i) -> p g i", g=8),
                                           in1=Umat.unsqueeze(1).broadcast_to([128, 8, 128]), op=ALU.mult),
          reads=[wsT_k, "consts"], writes=["wsTc"])
        A("dve", lambda e: e.tensor_tensor(out=diagD[:], in0=ident_f.unsqueeze(1).broadcast_to([128, 16, 128]),
                                           in1=pp[:, PP_DS:PP_DS + 16].unsqueeze(2).broadcast_to([128, 16, 128]), op=ALU.mult),
          reads=["consts", "pp"], writes=["diagD"])
        A("act", lambda e: e.activation(out=negA[:], in_=bp[:, BP_ALOG:BP_ALOG + 32], func=AF.Exp), reads=["bp"], writes=["negA"])
        A("dve", lambda e: e.tensor_scalar(out=negA[:], in0=negA[:], scalar1=-1.0, scalar2=None, op0=ALU.mult),
          reads=["negA"], writes=["negA"])
        A("act", lambda e: e.activation(out=cact[:], in_=pp[:, PP_C:PP_C + 8], func=AF.Silu), reads=["pp"], writes=["cact"])
        A("pool", lambda e: e.memset(mhalf[:], -0.5), writes=["mhalf"])
        A("pool", lambda e: e.memset(epsc[:], EPS), writes=["mhalf"])
        A("pool", lambda e: e.memset(Sst[:], 0.0), writes=["Sst"])
        A("pool", lambda e: e.memset(Sbf[:], 0.0), writes=["Sbf"])
        A("pool", lambda e: e.memset(tails[:], 0.0), writes=["tails"])

        big_f = big[:].rearrange("p a b -> p (a b)").bitcast(F32)
        nstage = (16 * T) // 4096
        assert nstage >= 1
        stage_keys = []
        tiles_per_stage = 32 // nstage
        for i in range(nstage):
            stage_keys.append([("big", t) for t in range(i * tiles_per_stage, (i + 1) * tiles_per_stage)])
        stage_i = [0]

        def get_stage():
            i = stage_i[0]
            stage_i[0] = (i + 1) % nstage
            return big_f[:, i * 4096:(i + 1) * 4096].rearrange("p (k n) -> p k n", k=8), stage_keys[i], ("stage", i)

        gB = [xres[:, 0, :], xres[:, 1 % NCH, :]]
        gBk = [("xres", 0), ("xres", 1 % NCH)]
        if NCH == 1:
            raise AssertionError("need NCH>=2")
        for nt in range(12):
            sec = nt // 2
            half = nt % 2
            stg, stg_keys, stg_sem = get_stage()
            dma_in("sp", stg, wmod_d[:, nt * 512:(nt + 1) * 512].rearrange("(k p) n -> p k n", p=128), stg_keys, stg_sem)
            bm, bm_k = r2k.get()
            dma_in("sp", bm[:], bmod_d[:, nt * 512:(nt + 1) * 512], [bm_k], ("bm", bm_k[1]))
            b = alloc()
            for k in range(8):
                A("pe", lambda e, k=k, b=b, stg=stg: e.matmul(ps[:, b, :], lhsT=cact[:, k:k + 1].broadcast_to([128, 128]),
                                                              rhs=stg[:, k, :], start=(k == 0), stop=(k == 7)),
                  reads=["cact"] + stg_keys, writes=pk(b))
            if sec in (2, 5):
                gi = 0 if sec == 2 else 1
                A("dve", lambda e, b=b, bm=bm, gi=gi, half=half: e.tensor_tensor(out=gB[gi][:, half * 512:(half + 1) * 512],
                                                                               in0=ps[:, b, :], in1=bm[:], op=ALU.add),
                  reads=pk(b) + [bm_k], writes=[gBk[gi]])
            else:
                col = {0: 0, 1: 1, 3: 2, 4: 3}[sec]
                tmp, tmp_k = r2k.get()
                A("dve", lambda e, b=b, bm=bm, tmp=tmp: e.tensor_tensor(out=tmp[:], in0=ps[:, b, :], in1=bm[:], op=ALU.add),
                  reads=pk(b) + [bm_k], writes=[tmp_k])
                A("dve", lambda e, tmp=tmp: e.tensor_tensor(out=tmp[:].rearrange("p (a i) -> p a i", a=4),
                                                            in0=tmp[:].rearrange("p (a i) -> p a i", a=4),
                                                            in1=ident_f.unsqueeze(1).broadcast_to([128, 4, 128]), op=ALU.mult),
                  reads=[tmp_k, "consts"], writes=[tmp_k])
                A("dve", lambda e, tmp=tmp, col=col, half=half: e.reduce_sum(out=modp[:, col, half * 4:(half + 1) * 4],
                                                                          in_=tmp[:].rearrange("p (a i) -> p a i", a=4), axis=AX.X),
                  reads=[tmp_k], writes=["modp"])
        for col in (1, 3):
            A("dve", lambda e, col=col: e.tensor_scalar(out=modp[:, col, :], in0=modp[:, col, :], scalar1=1.0, scalar2=None, op0=ALU.add),
              reads=["modp"], writes=["modp"])

        def fold_slot(slot, W, r0, c0, gi):
            stg, stg_keys, stg_sem = get_stage()
            dma_in("sp", stg, W[r0:r0 + 1024, c0:c0 + 512].rearrange("(k p) n -> p k n", p=128), stg_keys, stg_sem)
            wt, wt_k = wring.get()
            A("dve", lambda e: e.tensor_tensor(out=wt[:], in0=stg,
                                               in1=gB[gi][:, c0:c0 + 512].unsqueeze(1).broadcast_to([128, 8, 512]), op=ALU.mult),
              reads=stg_keys + [gBk[gi]], writes=[wt_k])
            A("sp", lambda e: e.dma_start(out=wsl_d[slot].rearrange("p (k n) -> p k n", k=8), in_=wt[:]),
              reads=[wt_k], writes=[("wsl", slot)], dma=True, semkey=("wsl", slot))
            wring.release(wt_k)

        for j in range(2):
            fold_slot(SLOTS["out"][0] + j, wout_d, 0, 512 * j, 0)
        for n in range(2):
            for kg in range(4):
                fold_slot(SLOTS["ff2"][0] + 4 * n + kg, wff2_d, 1024 * kg, 512 * n, 1)

        if DEBUG_BARRIER:
            S_.barrier()
        def wload(slot):
            wt, wt_k = wring.get()
            A("sp", lambda e: e.dma_start(out=wt[:].rearrange("p k n -> p (k n)"), in_=wsl_d[slot]),
              reads=[("wsl", slot)], writes=[wt_k], dma=True, semkey=wt_k)
            return wt, wt_k

        class Prefetch:
            def __init__(self):
                sl = [SLOTS["u"][0], SLOTS["u"][0] + 1, SLOTS["v"][0], SLOTS["v"][0] + 1]
                sl += [SLOTS["xbc"][0] + j for j in range(8)] + [SLOTS["z"][0] + j for j in range(4)]
                for j in range(2):
                    sl += [SLOTS["ga"][0] + j, SLOTS["gm"][0] + j, SLOTS["gb"][0] + j, SLOTS["ssm"][0] + 2 * j, SLOTS["ssm"][0] + 2 * j + 1]
                self.slots = sl
                self.loaded = {}
                self.nopen = 0

            def get(self, i, cap=3):
                for j in (i, i + 1, i + 2):
                    if j < len(self.slots) and j not in self.loaded and (j == i or self.nopen < cap):
                        self.loaded[j] = wload(self.slots[j])
                        self.nopen += 1
                return self.loaded[i]

            def done(self, i):
                wring.release(self.loaded[i][1])
                self.nopen -= 1

        def rstd_from_ss(ss, ss_k, n_feat, ring=None):
            r, r_k = (ring or rst).get()
            A("pool", lambda e: e.tensor_scalar(out=r[:, 0:1], in0=ss, scalar1=1.0 / n_feat, scalar2=EPS, op0=ALU.mult, op1=ALU.add),
              reads=[ss_k], writes=[r_k])
            A("pool", lambda e: e.tensor_tensor(out=r[:, 1:2], in0=r[:, 0:1], in1=mhalf[:], op=ALU.pow),
              reads=[r_k, "mhalf"], writes=[r_k])
            return r[:, 1:2], r_k

        jk2b = jk2[:].bitcast(BF16)

        def norm_multi(jobs):
            norm_tail(jobs, norm_head(jobs))

        def norm_head(jobs):
            st_ = []
            for (c, src, src_k, cs, cb, dst, dst_key, yth) in jobs:
                ring_s = rst_y if yth else rst
                ss, ss_k = ring_s.get()
                A("act", lambda e, src=src, ss=ss: e.activation(out=jk2b, in_=src, func=AF.Square, accum_out=ss[:, 0:1]),
                  reads=[src_k], writes=["jk2", ss_k])
                st_.append([ss, ss_k, ring_s])
            for i, (c, src, src_k, cs, cb, dst, dst_key, yth) in enumerate(jobs):
                ss, ss_k, ring_s = st_[i]
                rs, rs_k = rstd_from_ss(ss[:, 0:1], ss_k, D, ring_s)
                st_[i] += [rs, rs_k]
            for i, (c, src, src_k, cs, cb, dst, dst_key, yth) in enumerate(jobs):
                rs, rs_k = st_[i][3], st_[i][4]
                xb, xb_k = (r2k_y if yth else r2k).get()
                xbv = xb[:].bitcast(BF16)
                A("dve", lambda e, xbv=xbv, src=src, rs=rs: e.tensor_scalar(out=xbv, in0=src, scalar1=rs, scalar2=None, op0=ALU.mult),
                  reads=[src_k, rs_k], writes=[xb_k])
                st_[i] += [xbv, xb_k]
            return st_

        def norm_tail(jobs, st_):
            for i, (c, src, src_k, cs, cb, dst, dst_key, yth) in enumerate(jobs):
                xbv, xb_k = st_[i][5], st_[i][6]
                b = alloc_y() if yth else alloc()
                psb = ps[:, b, :].bitcast(BF16)
                for k in range(8):
                    A("pe", lambda e, k=k, psb=psb, xbv=xbv: e.transpose(out=psb[:, k * 128:(k + 1) * 128], in_=xbv[:, k * 128:(k + 1) * 128], identity=ident_bf),
                      reads=[xb_k, "ident_bf"], writes=pk(b))
                st_[i] += [b, psb]
            for i, (c, src, src_k, cs, cb, dst, dst_key, yth) in enumerate(jobs):
                b, psb = st_[i][7], st_[i][8]
                if yth:
                    for k in range(8):
                        A("dve", lambda e, k=k, psb=psb, dst=dst, c=c, cs=cs, cb=cb: e.tensor_scalar(out=dst[:, k, c * 128:(c + 1) * 128], in0=psb[:, k * 128:(k + 1) * 128],
                                                                                                     scalar1=modp[:, cs, k:k + 1], scalar2=modp[:, cb, k:k + 1],
                                                                                                     op0=ALU.mult, op1=ALU.add),
                          reads=pk(b) + ["modp"], writes=[dst_key])
                    continue
                for k in range(8):
                    A("act", lambda e, k=k, psb=psb, dst=dst, c=c, cs=cs, cb=cb: e.activation(out=dst[:, k, c * 128:(c + 1) * 128], in_=psb[:, k * 128:(k + 1) * 128], func=AF.Identity,
                                                                                              scale=modp[:, cs, k:k + 1], bias=modp[:, cb, k:k + 1]),
                      reads=pk(b) + ["modp"], writes=[dst_key])

        hT_keys = [("hT", c) for c in range(NCH)]

        def fm_matmuls(W, W_k, f, rhs_buf, rhs_keys, b):
            for k in range(8):
                A("pe", lambda e, k=k: e.matmul(ps[:, b, 0:T], lhsT=W[:, k, f * 128:(f + 1) * 128], rhs=rhs_buf[:, k, :],
                                                start=(k == 0), stop=(k == 7)),
                  reads=[W_k] + rhs_keys, writes=pk(b))

        def conv_tile_front(W, W_k, f, t):
            b = alloc()
            fm_matmuls(W, W_k, f, hT, hT_keys, b)
            raw, raw_k = rraw.get()
            acc, acc_k = rg.get()
            A("act", lambda e: e.activation(out=raw[:, 4:4 + T], in_=ps[:, b, 0:T], func=AF.Copy), reads=pk(b), writes=[raw_k])
            A("act", lambda e: e.activation(out=acc[:, 0:T], in_=ps[:, b, 0:T], func=AF.Identity,
                                            scale=pp[:, PP_CW + 4 * t + 3:PP_CW + 4 * t + 4], bias=pp[:, PP_CB + t:PP_CB + t + 1]),
              reads=pk(b) + ["pp"], writes=[acc_k])
            A("pool", lambda e: e.tensor_copy(out=raw[:, 0:4], in_=tails[:, t, :]), reads=[("tails", t)], writes=[raw_k])
            A("pool", lambda e: e.tensor_copy(out=tails[:, t, :], in_=raw[:, T:T + 4]), reads=[raw_k], writes=[("tails", t)])
            return raw, raw_k, acc, acc_k

        def conv_tap(raw, raw_k, acc, acc_k, t, kk):
            sh = 1 + kk
            A("dve", lambda e: e.scalar_tensor_tensor(out=acc[:, 0:T], in0=raw[:, sh:sh + T], scalar=pp[:, PP_CW + 4 * t + kk:PP_CW + 4 * t + kk + 1],
                                                      in1=acc[:, 0:T], op0=ALU.mult, op1=ALU.add),
              reads=[raw_k, acc_k, "pp"], writes=[acc_k])

        def phase_A_jobs(blk):
            t0 = blk * T
            jobs = []
            for c in range(NCH):
                xa, xa_k = r4k.get()
                dma_in("sp", xa[:], x_d[t0 + c * 128:t0 + (c + 1) * 128, :], [xa_k], xa_k)
                jobs.append((c, xa[:], xa_k, 1, 0, hT, ("hT", c), False))
            return jobs

        def phase_AB(blk, pf):
            t0 = blk * T
            for j in range(2):
                W, W_k = pf.get(j)
                for f in range(4):
                    ft = 4 * j + f
                    b = alloc()
                    fm_matmuls(W, W_k, f, hT, hT_keys, b)
                    A("act", lambda e, b=b, ft=ft: e.activation(out=guT[:, ft, :], in_=ps[:, b, 0:T], func=AF.Gelu_apprx_tanh),
                      reads=pk(b), writes=[("guT", ft)])
                    yield 0.5
                pf.done(j)
            Wv = [pf.get(2), pf.get(3)]
            for c in range(NCH):
                b = alloc(2)
                for j in range(2):
                    for k in range(8):
                        A("pe", lambda e, k=k, j=j, c=c, b=b, Wj=Wv[j][0]: e.matmul(ps[:, b + j, :], lhsT=hT[:, k, c * 128:(c + 1) * 128], rhs=Wj[:, k, :],
                                                                                   start=(k == 0), stop=(k == 7)),
                          reads=[Wv[j][1], ("hT", c)], writes=pk(b + j))
                gv, gv_k = r4k.get()
                A("act", lambda e, b=b, gv=gv: e.activation(out=gv[:], in_=ps[:, b:b + 2, :].rearrange("p a n -> p (a n)"), func=AF.Gelu_apprx_tanh),
                  reads=pk(b, 2), writes=[gv_k])
                ss, ss_k = rst.get()
                A("act", lambda e, gv=gv, ss=ss: e.activation(out=jk2b, in_=gv[:], func=AF.Square, accum_out=ss[:, 0:1]),
                  reads=[gv_k], writes=["jk2", ss_k])
                rs, rs_k = rstd_from_ss(ss[:, 0:1], ss_k, D)
                A("dve", lambda e, gv=gv, rs=rs, c=c: e.scalar_tensor_tensor(out=vn[:, c, :], in0=gv[:], scalar=rs, in1=bp[:, BP_GMNW:BP_GMNW + D],
                                                                             op0=ALU.mult, op1=ALU.mult),
                  reads=[gv_k, rs_k, "bp"], writes=[("vn", c)])
                yield 1.0
            pf.done(2)
            pf.done(3)
            for c in range(NCH):
                b = alloc()
                for k in range(8):
                    A("pe", lambda e, k=k, c=c, b=b: e.matmul(ps[:, b, 0:32], lhsT=hT[:, k, c * 128:(c + 1) * 128], rhs=wdt[:, k, :],
                                                              start=(k == 0), stop=(k == 7)),
                      reads=["wdt", ("hT", c)], writes=pk(b))
                A("dve", lambda e, c=c, b=b: e.tensor_tensor(out=dtx[:, c, :], in0=ps[:, b, 0:32], in1=bp[:, BP_DTB:BP_DTB + 32], op=ALU.add),
                  reads=pk(b) + ["bp"], writes=["dtx"])
            dtx_f = dtx[:].rearrange("p c h -> p (c h)")
            dtt_f = dtt[:].rearrange("p c h -> p (c h)")
            dtl_f = dtl[:].rearrange("p c h -> p (c h)")
            A("act", lambda e: e.activation(out=dtt_f, in_=dtx_f, func=AF.Abs), reads=["dtx"], writes=["dtt"])
            A("act", lambda e: e.activation(out=dtl_f, in_=dtt_f, func=AF.Exp, scale=-1.0), reads=["dtt"], writes=["dtl"])
            A("act", lambda e: e.activation(out=dtl_f, in_=dtl_f, func=AF.Ln, bias=1.0), reads=["dtl"], writes=["dtl"])
            A("dve", lambda e: e.scalar_tensor_tensor(out=dtt_f, in0=dtx_f, scalar=0.0, in1=dtl_f, op0=ALU.max, op1=ALU.add),
              reads=["dtx", "dtl"], writes=["dtt"])
            A("dve", lambda e: e.tensor_tensor(out=a_all[:], in0=dtt[:], in1=negA[:].unsqueeze(1).broadcast_to([128, NCH, 32]), op=ALU.mult),
              reads=["dtt", "negA"], writes=["a_all"])
            A("dve", lambda e: e.tensor_copy(out=a_hi[:], in_=a_all[:]), reads=["a_all"], writes=["a_hi"])
            yield 0.5
            for j in range(8):
                W, W_k = pf.get(4 + j)
                for fp in range(2):
                    tl = [4 * j + 2 * fp, 4 * j + 2 * fp + 1]
                    fr = [conv_tile_front(W, W_k, 2 * fp + i, tl[i]) for i in range(2)]
                    for kk in (2, 1, 0):
                        for i in range(2):
                            conv_tap(fr[i][0], fr[i][1], fr[i][2], fr[i][3], tl[i], kk)
                    for i in range(2):
                        A("act", lambda e, acc=fr[i][2], t=tl[i]: e.activation(out=big[:, t, :], in_=acc[:, 0:T], func=AF.Silu),
                          reads=[fr[i][3]], writes=[("big", tl[i])])
                    yield 1.2
                pf.done(4 + j)
            for j in range(4):
                W, W_k = pf.get(12 + j)
                for c in range(NCH):
                    b = alloc()
                    for k in range(8):
                        A("pe", lambda e, k=k, c=c, b=b, W=W: e.matmul(ps[:, b, :], lhsT=hT[:, k, c * 128:(c + 1) * 128], rhs=W[:, k, :],
                                                                      start=(k == 0), stop=(k == 7)),
                          reads=[W_k, ("hT", c)], writes=pk(b))
                    A("act", lambda e, b=b, c=c, j=j: e.activation(out=sz[:, c, j * 512:(j + 1) * 512], in_=ps[:, b, :], func=AF.Silu),
                      reads=pk(b), writes=[("sz", c)])
                    yield 0.3
                pf.done(12 + j)

        def chunk_pre(c):
            csl = slice(c * 128, (c + 1) * 128)
            b = alloc(2)
            for g in range(8):
                bb = b + g // 4
                col = (g % 4) * 128
                A("pe", lambda e, g=g, bb=bb, col=col: e.matmul(ps[:, bb, col:col + 128], lhsT=vn[:, c, g * 128:(g + 1) * 128], rhs=wsTc[:, g, :],
                                                               start=(g % 4 == 0), stop=False, skip_group_check=True),
                  reads=[("vn", c), "wsTc"], writes=pk(bb))
                for hl in range(2):
                    A("pe", lambda e, g=g, bb=bb, col=col, hl=hl: e.matmul(ps[:, bb, col:col + 128], lhsT=cbf[0:8, 0, g:g + 1].broadcast_to([8, 128]), rhs=bsmat[:, hl, :],
                                                                          start=False, stop=(hl == 1), skip_group_check=True),
                      reads=["ident_bf", "bsH"], writes=pk(bb))
            A("dve", lambda e: e.tensor_tensor(out=y_aT[:, :, csl], in0=ps[:, b:b + 2, :].rearrange("p a (g i) -> p (a g) i", g=4),
                                               in1=guT[:, :, csl], op=ALU.mult),
              reads=pk(b, 2) + [("guT", ft) for ft in range(8)], writes=[("y_aT", c)])
            b2 = alloc()
            A("pe", lambda e: e.matmul(ps[:, b2, 0:32], lhsT=Umat, rhs=a_all[:, c, :], start=True, stop=True, skip_group_check=True),
              reads=["consts", "a_all"], writes=pk(b2))
            A("pe", lambda e: e.matmul(ps[:, b2, 32:64], lhsT=ones_f, rhs=a_all[:, c, :], start=False, stop=True, skip_group_check=True),
              reads=["consts", "a_all"], writes=pk(b2))
            cl, ecl, wend = cl_buf, ecl_buf, wend_buf
            A("dve", lambda e: e.tensor_copy(out=cl[:], in_=ps[:, b2, 0:64]), reads=pk(b2), writes=["cl"])
            A("act", lambda e: e.activation(out=ecl[:], in_=cl[:], func=AF.Exp), reads=["cl"], writes=["ecl"])
            A("dve", lambda e: e.tensor_tensor(out=wend[:, 0:32], in0=cl[:, 32:64], in1=cl[:, 0:32], op=ALU.subtract), reads=["cl"], writes=["wend"])
            A("act", lambda e: e.activation(out=wend[:, 32:64], in_=wend[:, 0:32], func=AF.Exp), reads=["wend"], writes=["wend"])
            b3 = alloc(2)
            for t in range(16):
                bb = b3 + t // 8
                psb = ps[:, bb, :].bitcast(BF16)
                A("pe", lambda e, t=t, psb=psb: e.transpose(out=psb[:, (t % 8) * 128:(t % 8 + 1) * 128], in_=big[:, t, csl], identity=ident_bf),
                  reads=[("big", t), "ident_bf"], writes=pk(bb))
            for hb in range(2):
                psb = ps[:, b3 + hb, :].bitcast(BF16)
                A("dve", lambda e, psb=psb, hb=hb: e.tensor_tensor(out=xs_tok[:, hb * 1024:(hb + 1) * 1024].rearrange("p (h d) -> p h d", h=16),
                                                                   in0=psb.rearrange("p (h d) -> p h d", h=16),
                                                                   in1=dtt[:, c, hb * 16:(hb + 1) * 16].unsqueeze(2).broadcast_to([128, 16, 64]), op=ALU.mult),
                  reads=pk(b3 + hb) + ["dtt"], writes=["xs_tok"])
                A("pool", lambda e, hb=hb: e.tensor_tensor(out=xw[:, hb * 1024:(hb + 1) * 1024].rearrange("p (h d) -> p h d", h=16),
                                                           in0=xs_tok[:, hb * 1024:(hb + 1) * 1024].rearrange("p (h d) -> p h d", h=16),
                                                           in1=wend[:, 32 + hb * 16:32 + (hb + 1) * 16].unsqueeze(2).broadcast_to([128, 16, 64]), op=ALU.mult),
                  reads=["xs_tok", "wend"], writes=["xw"])
            b4 = alloc()
            psb4 = ps[:, b4, :].bitcast(BF16)
            for g in range(8):
                A("pe", lambda e, g=g: e.transpose(out=psb4[:, g * 128:(g + 1) * 128], in_=big[:, 16 + g, csl], identity=ident_bf),
                  reads=[("big", 16 + g), "ident_bf"], writes=pk(b4))
            A("act", lambda e: e.activation(out=B_tok[:], in_=psb4, func=AF.Copy), reads=pk(b4), writes=["B_tok"])
            b5 = alloc(2)
            for g in range(8):
                bb = b5 + g // 4
                col = (g % 4) * 128
                A("pe", lambda e, g=g, bb=bb, col=col: e.matmul(ps[:, bb, col:col + 128], lhsT=big[:, 16 + g, csl], rhs=big[:, 24 + g, csl],
                                                               start=(g % 4 == 0), stop=True, skip_group_check=True),
                  reads=[("big", 16 + g), ("big", 24 + g)], writes=pk(bb))
            A("dve", lambda e: e.tensor_tensor(out=cbm[:].rearrange("p (g i) -> p g i", g=8),
                                               in0=ps[:, b5:b5 + 2, :].rearrange("p a (g i) -> p (a g) i", g=4),
                                               in1=Umat.unsqueeze(1).broadcast_to([128, 8, 128]), op=ALU.mult),
              reads=pk(b5, 2) + ["consts"], writes=["cbm"])

        def build_aU(c, q):
            A("dve", lambda e: e.tensor_tensor(out=aUh[:].rearrange("p (h i) -> p h i", h=8),
                                               in0=a_hi[:, c, q * 8:(q + 1) * 8].unsqueeze(2).broadcast_to([128, 8, 128]),
                                               in1=Ubf.unsqueeze(1).broadcast_to([128, 8, 128]), op=ALU.mult),
              reads=["a_hi", "ident_bf"], writes=["aUh"])

        def stA(c, g, ctx):
            gl = g % 2
            b = alloc()
            A("pe", lambda e: e.matmul(ps[:, b, :], lhsT=Vbf, rhs=aUh[:, gl * 512:(gl + 1) * 512], start=True, stop=True),
              reads=["ident_bf", "aUh"], writes=pk(b))
            E, E_k = rE.get()
            A("act", lambda e: e.activation(out=E[:], in_=ps[:, b, :], func=AF.Exp), reads=pk(b), writes=[E_k])
            AT, AT_k = rAT.get()
            ATv = AT[:].rearrange("p (h i) -> p h i", h=4)
            A("dve", lambda e: e.tensor_tensor(out=ATv, in0=E[:].rearrange("p (h i) -> p h i", h=4),
                                               in1=cbm[:, g * 128:(g + 1) * 128].unsqueeze(1).broadcast_to([128, 4, 128]), op=ALU.mult),
              reads=[E_k, "cbm"], writes=[AT_k])
            ctx["AT"] = (ATv, AT_k)

        def stB(c, g, ctx):
            csl = slice(c * 128, (c + 1) * 128)
            ATv, AT_k = ctx["AT"]
            ecl = ecl_buf
            by = alloc()
            for i in range(2):
                t = 2 * g + i
                A("pe", lambda e, i=i, t=t: e.matmul(ps[:, by, i * 128:(i + 1) * 128], lhsT=big[:, t, csl], rhs=diagD[:, t, :],
                                                     start=(i == 0), stop=False, skip_group_check=True),
                  reads=[("big", t), "diagD"], writes=pk(by))
            for hh in range(4):
                h = 4 * g + hh
                A("pe", lambda e, hh=hh, h=h: e.matmul(ps[:, by, hh * 64:(hh + 1) * 64], lhsT=ATv[:, hh, :], rhs=xs_tok[:, h * 64:(h + 1) * 64],
                                                       start=False, stop=(hh == 3), skip_group_check=True),
                  reads=[AT_k, "xs_tok"], writes=pk(by))
            A("pe", lambda e: e.matmul(ps[:, by, 256:512], lhsT=big[:, 24 + g, csl], rhs=Sbf[:, g * 256:(g + 1) * 256],
                                       start=False, stop=True, skip_group_check=True),
              reads=[("big", 24 + g), ("Sbf", g)], writes=pk(by))
            bs_ = alloc()
            A("pe", lambda e: e.matmul(ps[:, bs_, 0:256], lhsT=B_tok[:, g * 128:(g + 1) * 128], rhs=xw[:, g * 256:(g + 1) * 256], start=True, stop=True),
              reads=["B_tok", "xw"], writes=pk(bs_))
            yi, yi_k = ryi.get()
            A("dve", lambda e: e.tensor_tensor(out=yi[:].rearrange("p (h d) -> p h d", h=4), in0=ps[:, by, 256:512].rearrange("p (h d) -> p h d", h=4),
                                               in1=ecl[:, 4 * g:4 * g + 4].unsqueeze(2).broadcast_to([128, 4, 64]), op=ALU.mult),
              reads=pk(by) + ["ecl"], writes=[yi_k])
            A("dve", lambda e: e.tensor_tensor(out=yi[:], in0=ps[:, by, 0:256], in1=yi[:], op=ALU.add), reads=pk(by) + [yi_k], writes=[yi_k])
            A("dve", lambda e: e.tensor_tensor(out=yi[:], in0=yi[:], in1=sz[:, c, g * 256:(g + 1) * 256], op=ALU.mult),
              reads=[yi_k, ("sz", c)], writes=[yi_k])
            Sg = Sst[:, g * 256:(g + 1) * 256]
            A("dve", lambda e: e.tensor_tensor(out=Sg.rearrange("p (h d) -> p h d", h=4), in0=Sg.rearrange("p (h d) -> p h d", h=4),
                                               in1=ecl[:, 32 + 4 * g:32 + 4 * g + 4].unsqueeze(2).broadcast_to([128, 4, 64]), op=ALU.mult),
              reads=[("Sst", g), "ecl"], writes=[("Sst", g)])
            A("dve", lambda e: e.tensor_tensor(out=Sg, in0=ps[:, bs_, 0:256], in1=Sg, op=ALU.add), reads=pk(bs_) + [("Sst", g)], writes=[("Sst", g)])
            A("act", lambda e: e.activation(out=Sbf[:, g * 256:(g + 1) * 256], in_=Sg, func=AF.Copy), reads=[("Sst", g)], writes=[("Sbf", g)])
            ctx["yi"] = (yi, yi_k)

        def stC(c, g, ctx):
            yi, yi_k = ctx["yi"]
            sg, sg_k = rst.get()
            A("act", lambda e: e.activation(out=jk2[:, 0:256], in_=yi[:], func=AF.Square, accum_out=sg[:, 0:1]), reads=[yi_k], writes=["jk2", sg_k])
            ctx["rs"] = rstd_from_ss(sg[:, 0:1], sg_k, 256)

        def stC2(c, g, ctx):
            yi, yi_k = ctx["yi"]
            rs, rs_k = ctx["rs"]
            gs, gs_k = rgs.get()
            A("act", lambda e: e.activation(out=gs[:], in_=yi[:], func=AF.Identity, scale=rs), reads=[yi_k, rs_k], writes=[gs_k])
            ctx["gs"] = (gs, gs_k)

        def stD(c, g, ctx):
            csl = slice(c * 128, (c + 1) * 128)
            gs, gs_k = ctx["gs"]
            bb = alloc()
            psb = ps[:, bb, :].bitcast(BF16)
            for i in range(2):
                A("pe", lambda e, i=i: e.transpose(out=psb[:, i * 128:(i + 1) * 128], in_=gs[:, i * 128:(i + 1) * 128], identity=ident_bf),
                  reads=[gs_k, "ident_bf"], writes=pk(bb))
            for i in range(2):
                t = 2 * g + i
                A("act", lambda e, t=t, i=i: e.activation(out=y_bT[:, t, csl], in_=psb[:, i * 128:(i + 1) * 128], func=AF.Identity,
                                                          scale=pp[:, PP_NW + t:PP_NW + t + 1]),
                  reads=pk(bb) + ["pp"], writes=[("y_bT", c)])

        def phase_C(blk):
            for c in range(NCH):
                chunk_pre(c)
                yield 2.0
                ctxs = [dict() for _ in range(8)]
                build_aU(c, 0)
                for step in range(8 + 3):
                    if 0 <= step - 2 < 8:
                        stC(c, step - 2, ctxs[step - 2])
                    if 0 <= step - 1 < 8:
                        stB(c, step - 1, ctxs[step - 1])
                    if step < 8:
                        stA(c, step, ctxs[step])
                        if step in (1, 3, 5):
                            build_aU(c, (step + 1) // 2)
                    yield 1.2
                    for _ in range(NFILL):
                        A("pe", lambda e: e.matmul(ps[:, 4, :], lhsT=ident_bf, rhs=wsTc[:, 0:4, :].rearrange("p g i -> p (g i)"), start=True, stop=True),
                          reads=["wsTc", "ident_bf"])
                    if 0 <= step - 2 < 8:
                        stC2(c, step - 2, ctxs[step - 2])
                    if 0 <= step - 3 < 8:
                        stD(c, step - 3, ctxs[step - 3])

        y_aT_keys = [("y_aT", c) for c in range(NCH)]
        y_bT_keys = [("y_bT", c) for c in range(NCH)]

        def phase_D(blk, pf, hook=None):
            if hook is not None:
                hook(0)
            for j in range(2):
                W, W_k = pf.get(16 + 5 * j)
                sa = []
                for f in range(4):
                    b = alloc()
                    fm_matmuls(W, W_k, f, hT, hT_keys, b)
                    s_, s_k = rg.get()
                    A("act", lambda e, b=b, s_=s_: e.activation(out=s_[:, 0:T], in_=ps[:, b, 0:T], func=AF.Sigmoid), reads=pk(b), writes=[s_k])
                    sa.append((s_, s_k))
                    yield 0.1
                pf.done(16 + 5 * j)
                W, W_k = pf.get(17 + 5 * j)
                for f in range(4):
                    b = alloc()
                    fm_matmuls(W, W_k, f, y_aT, y_aT_keys, b)
                    s_, s_k = sa[f]
                    A("dve", lambda e, b=b, s_=s_: e.tensor_tensor(out=s_[:, 0:T], in0=ps[:, b, 0:T], in1=s_[:, 0:T], op=ALU.mult),
                      reads=pk(b) + [s_k], writes=[s_k])
                    yield 0.1
                pf.done(17 + 5 * j)
                W, W_k = pf.get(18 + 5 * j)
                sbb = []
                for f in range(4):
                    b = alloc()
                    fm_matmuls(W, W_k, f, hT, hT_keys, b)
                    s_, s_k = rg.get()
                    A("act", lambda e, b=b, s_=s_: e.activation(out=s_[:, 0:T], in_=ps[:, b, 0:T], func=AF.Sigmoid), reads=pk(b), writes=[s_k])
                    sbb.append((s_, s_k))
                    yield 0.1
                pf.done(18 + 5 * j)
                if j == 1 and hook is not None:
                    hook(1)
                W0, W0_k = pf.get(19 + 5 * j)
                W1, W1_k = pf.get(20 + 5 * j)
                for f in range(4):
                    b = alloc()
                    for kk in range(16):
                        Wx, Wx_k = (W0, W0_k) if kk < 8 else (W1, W1_k)
                        A("pe", lambda e, kk=kk, Wx=Wx, f=f, b=b: e.matmul(ps[:, b, 0:T], lhsT=Wx[:, kk % 8, f * 128:(f + 1) * 128], rhs=y_bT[:, kk, :],
                                                                          start=(kk == 0), stop=(kk == 15)),
                          reads=[Wx_k] + y_bT_keys, writes=pk(b))
                    s_, s_k = sbb[f]
                    A("dve", lambda e, b=b, s_=s_: e.tensor_tensor(out=s_[:, 0:T], in0=ps[:, b, 0:T], in1=s_[:, 0:T], op=ALU.mult),
                      reads=pk(b) + [s_k], writes=[s_k])
                    a_, a_k = sa[f]
                    ft = 4 * j + f
                    A("dve", lambda e, s_=s_, a_=a_, ft=ft: e.tensor_tensor(out=guT[:, ft, :], in0=a_[:, 0:T], in1=s_[:, 0:T], op=ALU.add),
                      reads=[s_k, a_k], writes=[("guT", ft)])
                    yield 0.1
                pf.done(19 + 5 * j)
                pf.done(20 + 5 * j)
            if hook is not None:
                hook(2)

        mixT = guT
        mix_keys = [("guT", ft) for ft in range(8)]

        h2T_keys = [("h2T", c) for c in range(NCH)]

        def x_reread(blk):
            t0 = blk * T
            for c in range(NCH):
                dma_in("pool", xres[:, c, :], x_d[t0 + c * 128:t0 + (c + 1) * 128, :], [("xres", c)], ("xres", c))

        def phase_E_mm(blk, Wpre=None, x_done=False):
            if not x_done:
                x_reread(blk)
            for n in range(2):
                W, W_k = Wpre[n] if Wpre is not None else wload(SLOTS["out"][0] + n)
                for c in range(NCH):
                    b = alloc() if (n == 1 and c == NCH - 1) else alloc_y()
                    for k in range(8):
                        A("pe", lambda e, k=k, c=c, b=b, W=W: e.matmul(ps[:, b, :], lhsT=mixT[:, k, c * 128:(c + 1) * 128], rhs=W[:, k, :],
                                                                      start=(k == 0), stop=(k == 7)),
                          reads=[W_k] + mix_keys, writes=pk(b))
                    A("dve", lambda e, b=b, c=c, n=n: e.tensor_tensor(out=xres[:, c, n * 512:(n + 1) * 512], in0=ps[:, b, :], in1=xres[:, c, n * 512:(n + 1) * 512], op=ALU.add),
                      reads=pk(b) + [("xres", c)], writes=[("xres", c)])
                wring.release(W_k)
            return [(c, xres[:, c, :], ("xres", c), 3, 2, h2T, ("h2T", c), True) for c in range(NCH)]

        def phase_F(blk):
            for j in range(8):
                W, W_k = wload(SLOTS["ff1"][0] + j)
                for f in range(4):
                    ft = 4 * j + f
                    b = alloc_y()
                    fm_matmuls(W, W_k, f, h2T, h2T_keys, b)
                    r_, r_k = rT_y.get()
                    A("act", lambda e, b=b, r_=r_: e.activation(out=r_[:, 0:T], in_=ps[:, b, 0:T], func=AF.Relu), reads=pk(b), writes=[r_k])
                    A("act", lambda e, r_=r_, ft=ft: e.activation(out=fT[:, ft, :], in_=r_[:, 0:T], func=AF.Square),
                      reads=[r_k], writes=[("fT", ft)])
                    yield 1.0
                wring.release(W_k)

        def phase_G(blk):
            t0 = blk * T
            for n in range(2):
                b0 = alloc_y(NCH)
                for kg in range(4):
                    W, W_k = wload(SLOTS["ff2"][0] + 4 * n + kg)
                    for c in range(NCH):
                        for k in range(8):
                            kt = kg * 8 + k
                            A("pe", lambda e, k=k, kt=kt, c=c, W=W, b0=b0, kg=kg: e.matmul(ps[:, b0 + c, :], lhsT=fT[:, kt, c * 128:(c + 1) * 128], rhs=W[:, k, :],
                                                                                         start=(kg == 0 and k == 0), stop=(kg == 3 and k == 7)),
                              reads=[W_k, ("fT", kt)], writes=pk(b0 + c))
                        yield 1.9
                    wring.release(W_k)
                for c in range(NCH):
                    A("dve", lambda e, c=c, n=n, b0=b0: e.tensor_tensor(out=xres[:, c, n * 512:(n + 1) * 512], in0=ps[:, b0 + c, :], in1=xres[:, c, n * 512:(n + 1) * 512], op=ALU.add),
                      reads=pk(b0 + c) + [("xres", c)], writes=[("xres", c)])
            for c in range(NCH):
                ss, ss_k = rst_y.get()
                A("act", lambda e, ss=ss, c=c: e.activation(out=jk2b, in_=xres[:, c, :], func=AF.Square, accum_out=ss[:, 0:1]),
                  reads=[("xres", c)], writes=["jk2", ss_k])
                rs, rs_k = rstd_from_ss(ss[:, 0:1], ss_k, D, rst_y)
                A("dve", lambda e, rs=rs, c=c: e.scalar_tensor_tensor(out=xres[:, c, :], in0=xres[:, c, :], scalar=rs, in1=bp[:, BP_FNW:BP_FNW + D], op0=ALU.mult, op1=ALU.mult),
                  reads=[("xres", c), rs_k, "bp"], writes=[("xres", c)])
                A("pool", lambda e, c=c, t0=t0: e.dma_start(out=out_d[t0 + c * 128:t0 + (c + 1) * 128, :], in_=xres[:, c, :]),
                  reads=[("xres", c)], dma=True, semkey=("o", c))
                yield 1.0

        def run_all(gen):
            for _ in gen:
                pass

        def chainX(blk, pf, hook=None):
            yield from phase_AB(blk, pf)
            yield from phase_C(blk)
            yield from phase_D(blk, pf, hook)

        def chainY(blk, jobsE=()):
            for jb in jobsE:
                norm_multi([jb])
                yield 1.0
            yield from phase_F(blk)
            yield from phase_G(blk)

        W_TOTAL = 8 * 0.5 + 2 * 1.0 + 0.5 + 16 * 1.2 + 8 * 0.3 + NCH * (2.0 + 11 * 1.2) + 32 * 0.1
        Y_TOTAL = 32 * 1.0 + 16 * 1.9 + 2 * 1.0

        ystate = {"done": True}

        Y_DELAY = 2.0

        def interleave(X, Y):
            acc = 0.0
            started = False
            ynext = None
            ystate["done"] = Y is None
            if X is not None:
                for w in X:
                    acc += w * (Y_TOTAL / W_TOTAL) * 1.06
                    if Y is None:
                        continue
                    if not started:
                        if acc < Y_DELAY:
                            continue
                        started = True
                        acc = 0.0
                        ynext = next(Y, None)
                    while ynext is not None and acc >= ynext:
                        acc -= ynext
                        ynext = next(Y, None)
                    ystate["done"] = ynext is None
            if Y is not None:
                if not started:
                    ynext = next(Y, None)
                while ynext is not None:
                    ynext = next(Y, None)
            ystate["done"] = True

        if OVERLAP:
            early = {}

            def make_hook(blk):
                def hook(stage):
                    nxt = blk + 1 < NBLK
                    if stage == 0 and nxt:
                        early["jobsA"] = phase_A_jobs(blk + 1)
                    elif stage == 1:
                        early["Wout"] = [wload(SLOTS["out"][0] + n) for n in range(2)]
                        if nxt:
                            early["stA"] = norm_head(early["jobsA"])
                    elif stage == 2:
                        if nxt:
                            norm_tail(early.pop("jobsA"), early.pop("stA"))
                        if ystate["done"]:
                            x_reread(blk)
                            early["xre"] = True
                        if nxt:
                            pfn = Prefetch()
                            pfn.get(0, cap=2)
                            early["pf"] = pfn
                return hook

            norm_multi(phase_A_jobs(0))
            for blk in range(NBLK + 1):
                pf = early.pop("pf", None) or Prefetch()
                jobsE = []
                if blk < NBLK:
                    pf.get(0)
                if blk >= 1:
                    jobsE = phase_E_mm(blk - 1, early.pop("Wout"), early.pop("xre", False))
                interleave(chainX(blk, pf, make_hook(blk)) if blk < NBLK else None, chainY(blk - 1, jobsE) if blk >= 1 else None)
        else:
            for blk in range(NBLK):
                norm_multi(phase_A_jobs(blk))
                run_all(chainX(blk, Prefetch()))
                for jb in phase_E_mm(blk):
                    norm_multi([jb])
                run_all(chainY(blk))

        A("sp", None, writes=[("xres", c) for c in range(NCH)])
        if DEBUG_DUMP:
            lastd = [o for o in S_.ops if o.dma and isinstance(o.semkey, tuple) and o.semkey[0] == "dbg"]
            op = S_.add("sp", None)
            op.deps = [(p, True) for p in lastd]
        S_.emit(nc)
    print("ops:", {e: len(S_.streams[e]) for e in ENG_NAMES}, "sems:", S_.n_sems)
    return nc


def _host_layout(inputs, b):
    f = np.float32
    c = np.asarray(inputs["c"], f)[b]
    conv_w = np.asarray(inputs["conv_w"], f)[0]
    conv_b = np.asarray(inputs["conv_b"], f)[0]
    ssm_nw = np.asarray(inputs["ssm_norm_w"], f)[0]
    d_skip = np.asarray(inputs["d_skip"], f)[0]
    pp = np.zeros((128, PP_N), f)
    pp[:, PP_C:PP_C + 8] = c.reshape(8, 128).T
    cw = conv_w.reshape(4, 32, 128)
    pp[:, PP_CW:PP_CW + 128] = cw.transpose(2, 1, 0).reshape(128, 128)
    pp[:, PP_CB:PP_CB + 32] = conv_b.reshape(32, 128).T
    pp[:, PP_NW:PP_NW + 16] = ssm_nw.reshape(16, 128).T
    ch = np.arange(2048).reshape(16, 128).T
    pp[:, PP_DS:PP_DS + 16] = d_skip[ch // 64]
    bp = np.zeros((128, BP_N), f)
    bp[:, BP_DTB:BP_DTB + 32] = np.asarray(inputs["dt_bias"], f)[0][None, :]
    bp[:, BP_ALOG:BP_ALOG + 32] = np.asarray(inputs["a_log"], f)[0][None, :]
    bp[:, BP_GMNW:BP_GMNW + D] = np.asarray(inputs["gm_norm_w"], f)[0][None, :]
    bp[:, BP_FNW:BP_FNW + D] = np.asarray(inputs["final_norm_w"], f)[None, :]
    return pp, bp


def _consts():
    k = np.arange(128)
    ident = np.eye(128, dtype=np.float32)
    U = (k[:, None] <= k[None, :]).astype(np.float32)
    V = (k[:, None] > k[None, :]).astype(np.float32)
    ones = np.ones((128, 128), np.float32)
    return np.ascontiguousarray(np.stack([ident, U, V, ones], axis=1))


def run(inputs, S=None, T=256, cores=None, trace=False):
    f = np.float32
    x = np.asarray(inputs["x"], f)
    B = x.shape[0]
    if S is None:
        S = x.shape[1]
    cores = list(range(B)) if cores is None else cores
    nc = build_nc(S, T)
    consts = _consts()
    shared = {
        "w_mod": np.ascontiguousarray(np.asarray(inputs["w_mod"], f)[0]),
        "bmod": np.ascontiguousarray(np.broadcast_to(np.asarray(inputs["b_mod"], f)[0][None, :], (128, 6 * D))),
        "w_in": np.ascontiguousarray(np.asarray(inputs["w_in"], f)[0]),
        "w_gm": np.ascontiguousarray(np.asarray(inputs["w_branch_gm"], f)[0]),
        "w_ssm": np.ascontiguousarray(np.asarray(inputs["w_branch_ssm"], f)[0]),
        "w_out": np.ascontiguousarray(np.asarray(inputs["w_out"], f)[0]),
        "w_ff1": np.ascontiguousarray(np.asarray(inputs["w_ff1"], f)[0]),
        "w_ff2": np.ascontiguousarray(np.asarray(inputs["w_ff2"], f)[0]),
        "consts": consts,
        "bsrow": np.ascontiguousarray(np.asarray(inputs["gm_bs"], f)[0].reshape(8, 128)),
        "wsT": np.ascontiguousarray(np.asarray(inputs["gm_ws"], f)[0].transpose(2, 0, 1)),
    }
    in_maps = []
    for b in cores:
        pp, bp = _host_layout(inputs, b)
        m = dict(shared)
        m["x"] = np.ascontiguousarray(x[b, :S])
        m["pp"] = pp
        m["bp"] = bp
        in_maps.append(m)
    res = run_bass_kernel_spmd(nc, in_maps, core_ids=list(range(len(cores))), trace=trace)
    out = np.stack([np.asarray(r["out"], dtype=f) for r in res.results], axis=0)
    return out, res


def kernel(**inputs):
    out, _ = run(inputs)
    return out
```
